# Optimizing a Trainium2 kernel written in Bass

```python
import jax, jax.numpy as jnp
from jax import lax
import numpy as np

D_MODEL = 2048
BATCH = 4
SEQ = 2048
DEPTH = 2
DEC_BATCH = 128
DEC_SEQ = 8
PAST_LEN = 16384
PAGE_SIZE = 128

D_MIX = 2 * D_MODEL
D_CONV = D_MIX // 2
D_SSM = D_MIX - D_CONV
CONF_K = 31
SSM_HEAD_DIM = 64
SSM_HEADS = D_SSM // SSM_HEAD_DIM
SSM_GROUPS = 4
SSM_STATE = 128
SSM_CONV_K = 4
SSD_CHUNK = 128
D_XBC = D_SSM + 2 * SSM_GROUPS * SSM_STATE
D_IN = 2 * D_CONV + D_SSM + D_XBC + SSM_HEADS
D_FF = 5632
FFN_CONV_K = 3
N_MOD = 6
EPS = 1e-6

kernel_name = "hymba_conformer_ssd_convffn_adaln_step"


def rmsnorm(x, w):
    x32 = x.astype(jnp.float32)
    y = x32 * lax.rsqrt(jnp.mean(x32 * x32, axis=-1, keepdims=True) + EPS)
    return (y * w.astype(jnp.float32)).astype(x.dtype)


def layernorm(x, w, b):
    x32 = x.astype(jnp.float32)
    mu = jnp.mean(x32, axis=-1, keepdims=True)
    xc = x32 - mu
    y = xc * lax.rsqrt(jnp.mean(xc * xc, axis=-1, keepdims=True) + 1e-5)
    return (y * w.astype(jnp.float32) + b.astype(jnp.float32)).astype(x.dtype)


def causal_dwconv(u, buf, w, b):
    k = w.shape[0]
    full = jnp.concatenate([buf.astype(u.dtype), u], axis=1)
    out = lax.conv_general_dilated(
        full, w[:, None, :].astype(u.dtype), window_strides=(1,), padding='VALID',
        dimension_numbers=('NWC', 'WIO', 'NWC'), feature_group_count=u.shape[-1])
    new_buf = full[:, full.shape[1] - (k - 1):]
    return out + b.astype(u.dtype), new_buf


def ssd_scan(x, dt, a, bm, cm, d_skip, h0, chunk):
    bsz, L, H, P = x.shape
    G, N = bm.shape[2], bm.shape[3]
    R = H // G
    nc = L // chunk
    f32 = jnp.float32
    xr = x.reshape(bsz, nc, chunk, G, R, P).astype(f32)
    dtr = dt.reshape(bsz, nc, chunk, G, R)
    br = bm.reshape(bsz, nc, chunk, G, N).astype(f32)
    cr = cm.reshape(bsz, nc, chunk, G, N).astype(f32)
    acs = jnp.cumsum(dtr * a.reshape(G, R).astype(f32), axis=2)
    xdt = xr * dtr[..., None]
    seg = acs[:, :, :, None] - acs[:, :, None, :]
    causal = jnp.tril(jnp.ones((chunk, chunk), dtype=bool))[None, None, :, :, None, None]
    lmat = jnp.exp(jnp.where(causal, seg, -jnp.inf))
    cb = jnp.einsum('bcign,bcjgn->bcijg', cr, br)
    y_diag = jnp.einsum('bcijg,bcijgr,bcjgrp->bcigrp', cb, lmat, xdt)
    decay_end = jnp.exp(acs[:, :, -1:] - acs)
    chunk_states = jnp.einsum('bcjgn,bcjgr,bcjgrp->bcgrpn', br, decay_end, xdt)
    chunk_decay = jnp.exp(acs[:, :, -1])

    def step(h, inp):
        dec, st = inp
        return h * dec[..., None, None] + st, h

    h_fin, h_prev = lax.scan(step, h0.reshape(bsz, G, R, P, N).astype(f32),
                             (jnp.moveaxis(chunk_decay, 1, 0), jnp.moveaxis(chunk_states, 1, 0)))
    h_prev = jnp.moveaxis(h_prev, 0, 1)
    y_off = jnp.einsum('bcign,bcigr,bcgrpn->bcigrp', cr, jnp.exp(acs), h_prev)
    y = y_diag + y_off + xr * d_skip.reshape(G, R).astype(f32)[:, :, None]
    return y.reshape(bsz, L, H * P).astype(x.dtype), h_fin.reshape(bsz, H, P, N).astype(h0.dtype)


def hybrid_mixer(h, st_conf, st_sconv, st_ssm, w_in, conf_dw_w, conf_dw_b, conf_ln_w, conf_ln_b,
                 ssm_conv_w, ssm_conv_b, dt_bias, a_log, d_skip, ssm_norm_w, w_out, chunk):
    bsz, L, _ = h.shape
    p = h @ w_in
    o1 = 2 * D_CONV
    o2 = o1 + D_SSM
    o3 = o2 + D_XBC
    conf_in, z, xbc, dt_raw = p[..., :o1], p[..., o1:o2], p[..., o2:o3], p[..., o3:]
    u = conf_in[..., :D_CONV] * jax.nn.sigmoid(conf_in[..., D_CONV:])
    v, new_conf = causal_dwconv(u, st_conf, conf_dw_w, conf_dw_b)
    v = jax.nn.silu(layernorm(v, conf_ln_w, conf_ln_b))
    xbc, new_sconv = causal_dwconv(xbc, st_sconv, ssm_conv_w, ssm_conv_b)
    xbc = jax.nn.silu(xbc)
    gn = SSM_GROUPS * SSM_STATE
    xs = xbc[..., :D_SSM].reshape(bsz, L, SSM_HEADS, SSM_HEAD_DIM)
    bm = xbc[..., D_SSM:D_SSM + gn].reshape(bsz, L, SSM_GROUPS, SSM_STATE)
    cm = xbc[..., D_SSM + gn:].reshape(bsz, L, SSM_GROUPS, SSM_STATE)
    dt = jax.nn.softplus(dt_raw.astype(jnp.float32) + dt_bias.astype(jnp.float32))
    a = -jnp.exp(a_log.astype(jnp.float32))
    y, new_ssm = ssd_scan(xs, dt, a, bm, cm, d_skip, st_ssm, chunk)
    y = rmsnorm(y * jax.nn.silu(z), ssm_norm_w)
    out = jnp.concatenate([v, y], axis=-1) @ w_out
    return out, new_conf, new_sconv, new_ssm


def conv_ffn(h, st_f, w_up, dw_w, dw_b, w_down):
    up, new_f = causal_dwconv(h @ w_up, st_f, dw_w, dw_b)
    act = up[..., :D_FF] * jax.nn.silu(up[..., D_FF:])
    return act @ w_down, new_f


def run_trunk(x, c, st_conf, st_sconv, st_ssm, st_ffn, params):
    (ada_w, ada_b, norm_mix_w, norm_ffn_w, w_in, conf_dw_w, conf_dw_b, conf_ln_w, conf_ln_b,
     ssm_conv_w, ssm_conv_b, dt_bias, a_log, d_skip, ssm_norm_w, w_out,
     ffn_w_up, ffn_dw_w, ffn_dw_b, ffn_w_down, final_norm_w) = params
    L = x.shape[1]
    chunk = SSD_CHUNK if L % SSD_CHUNK == 0 else L
    new_conf, new_sconv, new_ssm, new_ffn = [], [], [], []
    for l in range(DEPTH):
        mod = jax.nn.silu(c) @ ada_w[l] + ada_b[l]
        sh_m, sc_m, g_m, sh_f, sc_f, g_f = jnp.split(mod[:, None, :], N_MOD, axis=-1)
        h = rmsnorm(x, norm_mix_w[l]) * (1 + sc_m) + sh_m
        out, nc_, ns_, nh_ = hybrid_mixer(
            h, st_conf[l], st_sconv[l], st_ssm[l], w_in[l], conf_dw_w[l], conf_dw_b[l],
            conf_ln_w[l], conf_ln_b[l], ssm_conv_w[l], ssm_conv_b[l], dt_bias[l], a_log[l],
            d_skip[l], ssm_norm_w[l], w_out[l], chunk)
        x = x + g_m * out
        h = rmsnorm(x, norm_ffn_w[l]) * (1 + sc_f) + sh_f
        f, nf_ = conv_ffn(h, st_ffn[l], ffn_w_up[l], ffn_dw_w[l], ffn_dw_b[l], ffn_w_down[l])
        x = x + g_f * f
        new_conf.append(nc_)
        new_sconv.append(ns_)
        new_ssm.append(nh_)
        new_ffn.append(nf_)
    y = rmsnorm(x, final_norm_w)
    return y, jnp.stack(new_conf), jnp.stack(new_sconv), jnp.stack(new_ssm), jnp.stack(new_ffn)


def setup_inputs(seed: int = 0) -> dict:
    key = jax.random.key(seed)
    ks = jax.random.split(key, 32)
    f32 = jnp.float32

    def nrm(k, shape, scale):
        return jax.random.normal(k, shape, f32) * scale

    dt0 = jnp.exp(jax.random.uniform(ks[20], (DEPTH, SSM_HEADS), f32,
                                     jnp.log(0.001), jnp.log(0.1)))
    dt_bias = dt0 + jnp.log(-jnp.expm1(-dt0))
    a_log = jnp.log(jax.random.uniform(ks[21], (DEPTH, SSM_HEADS), f32, 1.0, 16.0))
    return {
        "x_prompt": nrm(ks[0], (BATCH, SEQ, D_MODEL), 1.0),
        "x_sample": nrm(ks[1], (DEC_BATCH, DEC_SEQ, D_MODEL), 1.0),
        "c_prompt": nrm(ks[2], (BATCH, D_MODEL), 1.0),
        "c_sample": nrm(ks[3], (DEC_BATCH, D_MODEL), 1.0),
        "state_conf_conv": nrm(ks[4], (DEPTH, DEC_BATCH, CONF_K - 1, D_CONV), 0.5),
        "state_ssm_conv": nrm(ks[5], (DEPTH, DEC_BATCH, SSM_CONV_K - 1, D_XBC), 1.0),
        "state_ssm": nrm(ks[6], (DEPTH, DEC_BATCH, SSM_HEADS, SSM_HEAD_DIM, SSM_STATE), 0.05),
        "state_ffn_conv": nrm(ks[7], (DEPTH, DEC_BATCH, FFN_CONV_K - 1, 2 * D_FF), 1.0),
        "ada_w": nrm(ks[8], (DEPTH, D_MODEL, N_MOD * D_MODEL), 0.5 * D_MODEL ** -0.5),
        "ada_b": nrm(ks[9], (DEPTH, N_MOD * D_MODEL), 0.02),
        "norm_mix_w": 1.0 + nrm(ks[10], (DEPTH, D_MODEL), 0.02),
        "norm_ffn_w": 1.0 + nrm(ks[11], (DEPTH, D_MODEL), 0.02),
        "w_in": nrm(ks[12], (DEPTH, D_MODEL, D_IN), D_MODEL ** -0.5),
        "conf_dw_w": nrm(ks[13], (DEPTH, CONF_K, D_CONV), CONF_K ** -0.5),
        "conf_dw_b": nrm(ks[14], (DEPTH, D_CONV), 0.02),
        "conf_ln_w": 1.0 + nrm(ks[15], (DEPTH, D_CONV), 0.02),
        "conf_ln_b": nrm(ks[16], (DEPTH, D_CONV), 0.02),
        "ssm_conv_w": nrm(ks[17], (DEPTH, SSM_CONV_K, D_XBC), SSM_CONV_K ** -0.5),
        "ssm_conv_b": nrm(ks[18], (DEPTH, D_XBC), 0.02),
        "dt_bias": dt_bias,
        "a_log": a_log,
        "d_skip": 1.0 + nrm(ks[19], (DEPTH, SSM_HEADS), 0.1),
        "ssm_norm_w": 1.0 + nrm(ks[22], (DEPTH, D_SSM), 0.02),
        "w_out": nrm(ks[23], (DEPTH, D_MIX, D_MODEL), D_MIX ** -0.5),
        "ffn_w_up": nrm(ks[24], (DEPTH, D_MODEL, 2 * D_FF), D_MODEL ** -0.5),
        "ffn_dw_w": nrm(ks[25], (DEPTH, FFN_CONV_K, 2 * D_FF), FFN_CONV_K ** -0.5),
        "ffn_dw_b": nrm(ks[26], (DEPTH, 2 * D_FF), 0.02),
        "ffn_w_down": nrm(ks[27], (DEPTH, D_FF, D_MODEL), D_FF ** -0.5),
        "final_norm_w": 1.0 + nrm(ks[28], (D_MODEL,), 0.02),
    }


def reference(x_prompt, x_sample, c_prompt, c_sample, state_conf_conv, state_ssm_conv, state_ssm,
              state_ffn_conv, ada_w, ada_b, norm_mix_w, norm_ffn_w, w_in, conf_dw_w, conf_dw_b,
              conf_ln_w, conf_ln_b, ssm_conv_w, ssm_conv_b, dt_bias, a_log, d_skip, ssm_norm_w,
              w_out, ffn_w_up, ffn_dw_w, ffn_dw_b, ffn_w_down, final_norm_w):
    params = (ada_w, ada_b, norm_mix_w, norm_ffn_w, w_in, conf_dw_w, conf_dw_b, conf_ln_w,
              conf_ln_b, ssm_conv_w, ssm_conv_b, dt_bias, a_log, d_skip, ssm_norm_w, w_out,
              ffn_w_up, ffn_dw_w, ffn_dw_b, ffn_w_down, final_norm_w)
    bp = x_prompt.shape[0]
    dtp = x_prompt.dtype
    z_conf = jnp.zeros((DEPTH, bp, CONF_K - 1, D_CONV), dtp)
    z_sconv = jnp.zeros((DEPTH, bp, SSM_CONV_K - 1, D_XBC), dtp)
    z_ssm = jnp.zeros((DEPTH, bp, SSM_HEADS, SSM_HEAD_DIM, SSM_STATE), state_ssm.dtype)
    z_ffn = jnp.zeros((DEPTH, bp, FFN_CONV_K - 1, 2 * D_FF), dtp)
    y_prompt, p_conf, p_sconv, p_ssm, p_ffn = run_trunk(
        x_prompt, c_prompt, z_conf, z_sconv, z_ssm, z_ffn, params)
    y_sample, s_conf, s_sconv, s_ssm, s_ffn = run_trunk(
        x_sample, c_sample, state_conf_conv, state_ssm_conv, state_ssm, state_ffn_conv, params)
    return (y_prompt, y_sample, p_conf, p_sconv, p_ssm, p_ffn, s_conf, s_sconv, s_ssm, s_ffn)
```

```python
import bisect
from contextlib import ExitStack

import numpy as np
import concourse.bass as bass
import concourse.mybir as mybir
from concourse.bass_utils import run_bass_kernel_spmd

F32 = mybir.dt.float32
BF16 = mybir.dt.bfloat16
ALU = mybir.AluOpType
AF = mybir.ActivationFunctionType

D = 2048
KC = 16
SEQ = 2048
NBP = 512
NPB = SEQ // NBP
NS = 16
TS = 8
DEPTH = 2
D_CONV = 2048
D_SSM = 2048
D_XBC = 3072
HEADS = 32
HP = 64
GROUPS = 4
NST = 128
D_FF = 5632
FKC = D_FF // 128
CONF_K = 31
SSM_K = 4
FFN_K = 3
EPS = 1e-6
NSLOT = 5

RAW_WINDOW = 2
SAME_ENG_SYNC = {"pe": False, "act": True, "dve": True, "pool": True, "sp": False}

_off = {}
_o = 0
for _n, _w in [("nwm", 16), ("nwf", 16), ("cw", 16 * 31), ("cb", 16), ("lnw", 16), ("lnb", 16),
               ("scw", 24 * 4), ("scb", 24), ("dtb", 32), ("alog", 32), ("dsk", 32), ("snw", 16),
               ("fcw", 88 * 3), ("fcb", 88), ("adab", 96)]:
    _off[_n] = (_o, _w)
    _o += _w
NPAR = _o
C_ID, C_TRI, C_BTRI, C_SAME, C_SEQM = 0, 128, 256, 384, 512
NCONST = 512 + 16


class T:
    __slots__ = ("name", "lw", "rd", "parts", "psum")

    def __init__(self, name, nparts=0):
        self.name = name
        self.lw = None
        self.rd = {}
        self.psum = False
        self.parts = [T("%s.%d" % (name, i)) for i in range(nparts)] if nparts else None


def psum_bank_tile(name):
    leaf = T(name + ".b")
    leaf.psum = True
    t = T(name)
    t.parts = [leaf] * 4
    return t


def _leaves(ts):
    out = []
    for t in ts:
        if t is None:
            continue
        if t.parts:
            out.extend(t.parts)
        else:
            out.append(t)
    return out


class Prog:
    def __init__(self, nc):
        self.nc = nc
        self.streams = {k: [] for k in ("pe", "act", "dve", "pool", "sp")}
        self.flag = {k: set() for k in self.streams}
        self.dmacnt = {}
        self.total_sems = set()
        self.recent = {k: [] for k in self.streams}

    def _deps(self, eng, r, w):
        deps = {}
        raw = set()

        def add(ev, is_raw):
            k = (ev[0], ev[1])
            if is_raw:
                raw.add(k)
            if ev[2] is None:
                deps[k] = None
            elif k in deps and deps[k] is None:
                pass
            elif deps.get(k, -1) < ev[2]:
                deps[k] = ev[2]

        for t in r:
            if t.lw is not None:
                add(t.lw, True)
            if t.psum:
                for k, v in t.rd.items():
                    if not (k[0] == "e" and k[1] == eng):
                        add((k[0], k[1], v), False)
        for t in w:
            if t.lw is not None:
                add(t.lw, False)
            for k, v in t.rd.items():
                add((k[0], k[1], v), False)
        out = []
        for (kind, key), v in deps.items():
            if kind == "e":
                if key == eng:
                    if not SAME_ENG_SYNC[eng] or (kind, key) not in raw or v not in self.recent[eng]:
                        continue
                self.flag[key].add(v)
            out.append((kind, key, v))
        return out

    def _register(self, ev, r, w):
        k = (ev[0], ev[1])
        for t in r:
            if ev[2] is None:
                t.rd[k] = None
            elif k in t.rd and t.rd[k] is None:
                pass
            elif t.rd.get(k, -1) < ev[2]:
                t.rd[k] = ev[2]
        for t in w:
            t.lw = ev
            t.rd = {}

    def op(self, eng, fn, r=(), w=()):
        r = _leaves(r)
        w = _leaves(w)
        idx = len(self.streams[eng])
        deps = self._deps(eng, r, w)
        self._register(("e", eng, idx), r, w)
        self.streams[eng].append((fn, deps, None))
        if fn is not None:
            self.recent[eng] = (self.recent[eng] + [idx])[-RAW_WINDOW:]

    def dma(self, q, fn, sem, r=(), w=()):
        r = _leaves(r)
        w = _leaves(w)
        deps = self._deps(q, r, w)
        if sem in self.total_sems:
            deps = [d for d in deps if not (d[0] == "d" and d[1] == sem)]
        self.dmacnt[sem] = self.dmacnt.get(sem, 0) + 16
        val = None if sem in self.total_sems else self.dmacnt[sem]
        self._register(("d", sem, val), r, w)
        self.streams[q].append((fn, deps, sem))

    def barrier(self):
        deps = []
        for eng in ("pe", "act", "dve", "pool"):
            st = self.streams[eng]
            for idx in range(len(st) - 1, -1, -1):
                fn, _, dsn = st[idx]
                if fn is not None and dsn is None:
                    deps.append(("e", eng, idx))
                    self.flag[eng].add(idx)
                    break
        for sem, cnt in self.dmacnt.items():
            deps.append(("d", sem, cnt))
        for eng in self.streams:
            self.streams[eng].append((None, list(deps), None))

    def emit(self, es):
        nc = self.nc
        esem = {k: es.enter_context(nc.semaphore("e_" + k)) for k in ("pe", "act", "dve", "pool")}
        dsem = {k: es.enter_context(nc.semaphore("d_" + k)) for k in self.dmacnt}
        flagged = {k: sorted(v) for k, v in self.flag.items()}
        block = es.enter_context(nc.Block())
        streams = self.streams
        dmacnt = self.dmacnt

        def run(engname, e):
            waited = {}
            fl = flagged[engname]
            fls = self.flag[engname]
            for idx, (fn, deps, dsn) in enumerate(streams[engname]):
                for kind, key, v in deps:
                    if kind == "e":
                        sem = esem[key]
                        val = bisect.bisect_right(flagged[key], v)
                    else:
                        sem = dsem[key]
                        val = dmacnt[key] if v is None else v
                    wk = (kind, key)
                    if waited.get(wk, 0) >= val:
                        continue
                    waited[wk] = val
                    e.wait_ge(sem, val)
                if fn is None:
                    continue
                ins = getattr(e, fn[0])(**fn[1])
                if dsn is not None:
                    ins.then_inc(dsem[dsn], 16)
                elif idx in fls:
                    ins.then_inc(esem[engname], 1)

        @block.tensor
        def _(e):
            run("pe", e)

        @block.scalar
        def _(e):
            run("act", e)

        @block.vector
        def _(e):
            run("dve", e)

        @block.gpsimd
        def _(e):
            run("pool", e)

        @block.sync
        def _(e):
            run("sp", e)


def build_program():
    nc = bass.Bass("TRN2", target_bir_lowering=False)

    def din(name, shape):
        return nc.dram_tensor(name, list(shape), F32, kind="ExternalInput").ap()

    def dout(name, shape):
        return nc.dram_tensor(name, list(shape), F32, kind="ExternalOutput").ap()

    xTp = din("xTp", [128, KC, SEQ])
    xTs = din("xTs", [128, KC, 128])
    cT = din("cT", [128, KC, 17])
    consts = din("consts", [128, NCONST])
    par = din("par", [DEPTH, 128, NPAR])
    fnw = din("fnw", [128, KC])
    w_in = din("w_in", [DEPTH, 72, 128, 16, 128])
    w_dt = din("w_dt", [DEPTH, 128, 16, 32])
    w_out = din("w_out", [DEPTH, 16, 128, 32, 128])
    w_up = din("w_up", [DEPTH, 88, 128, 16, 128])
    w_dn = din("w_dn", [DEPTH, 16, 128, 44, 128])
    ada = din("ada", [DEPTH, 96, 128, 16, 128])
    hconf = din("hconf", [DEPTH, 128, 16, NS, 30])
    hssm = din("hssm", [DEPTH, 128, 24, NS, 3])
    hffn = din("hffn", [DEPTH, 128, 88, NS, 2])
    h0T = din("h0T", [DEPTH, NS, 128, D_SSM])

    yTp = dout("yTp", [128, KC, SEQ])
    yTs = dout("yTs", [128, KC, 128])
    oconf_p = dout("oconf_p", [DEPTH, 128, 16, 30])
    ossm_p = dout("ossm_p", [DEPTH, 128, 24, 3])
    ostate_p = dout("ostate_p", [DEPTH, 128, D_SSM])
    offn_p = dout("offn_p", [DEPTH, 128, 88, 2])
    ohist_s = dout("ohist_s", [DEPTH, 128, 16, NS, 30])
    onew_s = dout("onew_s", [DEPTH, 128, 16, 128])
    ossm_s = dout("ossm_s", [DEPTH, 128, 24, NS, 3])
    ostate_s = dout("ostate_s", [DEPTH, NS, 128, D_SSM])
    offn_s = dout("offn_s", [DEPTH, 128, 88, NS, 2])

    def dcache(name, shape):
        return nc.dram_tensor(name, list(shape), BF16, kind="Internal").ap()

    c_w_in = dcache("c_w_in", [DEPTH, 72, 128, 16, 128])
    c_w_out = dcache("c_w_out", [DEPTH, 16, 128, 32, 128])
    c_w_up = dcache("c_w_up", [DEPTH, 88, 128, 16, 128])
    c_w_dn = dcache("c_w_dn", [DEPTH, 16, 128, 44, 128])

    P = Prog(nc)
    P.total_sems.add("par")
    P.total_sems.add("o_fin")

    with ExitStack() as es:
        def sb(name, shape, dt=F32):
            return es.enter_context(nc.sbuf_tensor("s_" + name, list(shape), dt))

        cst = sb("cst", [128, NCONST]); t_cst = T("cst")
        ident = cst[:, C_ID:C_ID + 128]
        tri = cst[:, C_TRI:C_TRI + 128]
        btri = cst[:, C_BTRI:C_BTRI + 128]
        same = cst[:, C_SAME:C_SAME + 128]
        seqm = cst[:, C_SEQM:C_SEQM + 16]
        ones_f = sb("ones_f", [128, 128]); t_ones = T("ones")
        ones_b = sb("ones_b", [128, 128], BF16)
        eps_t = sb("eps_t", [128, 2])
        ident_b = sb("ident_b", [128, 2, 128], BF16)
        part = [sb("par%d" % l, [128, NPAR]) for l in range(DEPTH)]; t_par = T("par")
        fnw_t = sb("fnw", [128, KC])
        a_bc = [sb("a_bc%d" % l, [128, 32]) for l in range(DEPTH)]
        modT = [sb("modT%d" % l, [128, 96, 17]) for l in range(DEPTH)]
        t_mod = [T("mod%d" % l) for l in range(DEPTH)]
        Wm = [sb("Wm%d" % l, [128, 16, 17]) for l in range(DEPTH)]
        Wf = [sb("Wf%d" % l, [128, 16, 17]) for l in range(DEPTH)]

        def pcol(l, name, a=0, b=None):
            o, w = _off[name]
            b = w if b is None else b
            return part[l][:, o + a:o + b]

        halo_cb = [sb("halo_cb%d" % l, [128, 16, 30], BF16) for l in range(DEPTH)]
        halo_c32 = [sb("halo_c32%d" % l, [128, 16, 30]) for l in range(DEPTH)]
        halo_s = [sb("halo_s%d" % l, [128, 24, 3]) for l in range(DEPTH)]
        halo_f = [sb("halo_f%d" % l, [128, 88, 2]) for l in range(DEPTH)]
        hstate = sb("hstate", [128, D_SSM])
        t_halo_cb = [T("halo_cb%d" % l, 16) for l in range(DEPTH)]
        t_halo_c32 = [T("halo_c32%d" % l, 16) for l in range(DEPTH)]
        t_halo_s = [T("halo_s%d" % l, 24) for l in range(DEPTH)]
        t_halo_f = [T("halo_f%d" % l, 88) for l in range(DEPTH)]
        t_hstate = T("hstate", 4)
        t_ost = [T("ost%d" % l) for l in range(DEPTH)]

        ring = sb("ring", [128, NSLOT, 16, 128], BF16); t_ring = [T("ring%d" % i) for i in range(NSLOT)]
        wdt_t = sb("wdt", [128, 16, 32], BF16); t_wdt = T("wdt")
        rstd = sb("rstd", [128, NBP]); t_rstd = T("rstd")
        mu_t = sb("mu", [128, NBP]); t_mu = T("mu")
        tmpA = [sb("tmpA%d" % i, [128, NBP]) for i in range(3)]; t_tmpA = [T("tmpA%d" % i) for i in range(3)]
        tmpB = [sb("tmpB%d" % i, [128, NBP], BF16) for i in range(2)]; t_tmpB = [T("tmpB%d" % i) for i in range(2)]
        dtraw = [sb("dtraw%d" % i, [128, 32]) for i in range(NBP // 128)]; t_dtraw = [T("dtraw%d" % i) for i in range(NBP // 128)]

        R1W = 27648
        R2W = 4608
        R1 = sb("R1", [128, R1W])
        R2 = sb("R2", [128, R2W])

        class Carver:
            def __init__(self, base, words):
                self.base = base
                self.words = words
                self.off = 0

            def take(self, shape, dt=F32):
                n = 1
                for d_ in shape[1:]:
                    n *= d_
                words = n if dt == F32 else (n + 1) // 2
                assert self.off + words <= self.words, ("arena overflow", self.off, words, self.words)
                v = self.base[:, self.off:self.off + words]
                self.off += words
                if dt != F32:
                    v = v.bitcast(dt)[:, 0:n]
                if len(shape) == 3:
                    v = v.rearrange("p (a b) -> p a b", a=shape[1])
                elif len(shape) == 4:
                    v = v.rearrange("p (a b c) -> p a b c", a=shape[1], b=shape[2])
                return v

        banks = [es.enter_context(nc.psum_tensor("bank%d" % i, [128, 512], F32)) for i in range(8)]
        t_bank = [psum_bank_tile("bank%d" % i) for i in range(8)]

        wctr = [0]

        wmode = {"first": True}

        def wload(src_ap, kt=16, cache=None):
            i = wctr[0] % NSLOT
            wctr[0] += 1
            if cache is None or wmode["first"]:
                P.dma("pool", ("dma_start", dict(out=ring[:, i, 0:kt, :], in_=src_ap)), "ring%d" % i, w=[t_ring[i]])
                if cache is not None:
                    P.dma("sp", ("dma_start", dict(out=cache, in_=ring[:, i, 0:kt, :])), "rc%d" % i, r=[t_ring[i]])
            else:
                P.dma("pool", ("dma_start", dict(out=ring[:, i, 0:kt, :], in_=cache)), "ring%d" % i, w=[t_ring[i]])
            return i

        def proj(bank_i, units, rhs_list, n, rd):
            tot = len(rhs_list)
            k = 0
            per_k = (len(rd) == tot)
            for (u, kts) in units:
                for kt in range(kts):
                    rhs = rhs_list[k]
                    P.op("pe", ("matmul", dict(out=banks[bank_i][:, 0:n], lhsT=ring[:, u, kt, :], rhs=rhs, start=(k == 0), stop=(k == tot - 1))),
                         r=[t_ring[u]] + ([rd[k]] if per_k else rd), w=[t_bank[bank_i]])
                    k += 1

        cTt = sb("cTt", [128, KC, 17])
        scT = sb("scT", [128, KC, 17], BF16)
        t_scT = T("scT")
        P.dma("sp", ("dma_start", dict(out=cst[:], in_=consts)), "par", w=[t_cst])
        for l in range(DEPTH):
            P.dma("sp", ("dma_start", dict(out=part[l][:], in_=par[l])), "par", w=[t_par])
        P.dma("sp", ("dma_start", dict(out=fnw_t[:], in_=fnw)), "par", w=[t_par])
        P.dma("sp", ("dma_start", dict(out=cTt[:], in_=cT)), "par", w=[t_scT])
        P.op("dve", ("memset", dict(ap=ones_f[:], constant=1.0)), w=[t_ones])
        P.op("dve", ("memset", dict(ap=ones_b[:], constant=1.0)), w=[t_ones])
        P.op("dve", ("tensor_copy", dict(out=ident_b[:, 0, :], in_=ident)), r=[t_cst], w=[t_ones])
        P.op("dve", ("tensor_copy", dict(out=ident_b[:, 1, :], in_=ident)), r=[t_cst], w=[t_ones])
        P.op("dve", ("memset", dict(ap=eps_t[:, 0:1], constant=EPS)), w=[t_ones])
        P.op("dve", ("memset", dict(ap=eps_t[:, 1:2], constant=1e-5)), w=[t_ones])
        for l in range(DEPTH):
            P.op("dve", ("memset", dict(ap=halo_cb[l][:], constant=0.0)), w=[t_halo_cb[l]])
            P.op("dve", ("memset", dict(ap=halo_c32[l][:], constant=0.0)), w=[t_halo_c32[l]])
            P.op("dve", ("memset", dict(ap=halo_s[l][:], constant=0.0)), w=[t_halo_s[l]])
            P.op("dve", ("memset", dict(ap=halo_f[l][:], constant=0.0)), w=[t_halo_f[l]])
            P.op("act", ("activation", dict(out=a_bc[l][:], in_=pcol(l, "alog"), func=AF.Exp)), r=[t_par], w=[t_par])
            P.op("dve", ("tensor_scalar", dict(out=a_bc[l][:], in0=a_bc[l][:], scalar1=-1.0, scalar2=None, op0=ALU.mult)), r=[t_par], w=[t_par])
        P.op("act", ("activation", dict(out=scT[:], in_=cTt[:], func=AF.Silu)), r=[t_scT], w=[t_scT])
        def compute_mod(l):
            for m in range(96):
                u = wload(ada[l, m])
                b = m % 6
                proj(b, [(u, 16)], [scT[:, kt, :] for kt in range(16)], 17, [t_scT])
                P.op("act", ("activation", dict(out=modT[l][:, m, :], in_=banks[b][:, 0:17], func=AF.Identity,
                                                bias=pcol(l, "adab", m, m + 1), scale=1.0)),
                     r=[t_bank[b], t_par], w=[t_mod[l]])
            for (Wt, base, nm) in ((Wm[l], 16, "nwm"), (Wf[l], 64, "nwf")):
                P.op("dve", ("tensor_scalar", dict(out=Wt[:], in0=modT[l][:, base:base + 16, :], scalar1=1.0, scalar2=None, op0=ALU.add)),
                     r=[t_mod[l]], w=[t_mod[l]])
                P.op("dve", ("tensor_tensor", dict(out=Wt[:], in0=Wt[:], in1=pcol(l, nm).unsqueeze(2).to_broadcast([128, 16, 17]), op=ALU.mult)),
                     r=[t_mod[l], t_par], w=[t_mod[l]])
        compute_mod(0)
        P.barrier()

        def make_block_fns(kind):
            n = NBP if kind == "P" else 128
            NTT = n // 128
            pfx = kind
            c1 = Carver(R1, R1W)
            xT = c1.take([128, KC, n]); t_xT = T(pfx + "xT", KC)
            hT = c1.take([128, KC, n], BF16); t_hT = T(pfx + "hT", KC)
            BIG = c1.take([128, 48, n], BF16); t_BIG = T(pfx + "BIG", 48)
            BT = c1.take([128, 4, n], BF16); t_BT = T(pfx + "BT", 4)
            CT = c1.take([128, 4, n], BF16); t_CT = T(pfx + "CT", 4)
            Btok = c1.take([128, NTT, 512], BF16); t_Btok = T(pfx + "Btok", NTT)
            if kind == "S":
                h0f = [c1.take([128, D_SSM]) for i in range(2)]; t_h0f = [T("h0f%d" % i, 4) for i in range(2)]
                h0b = [c1.take([128, D_SSM], BF16) for i in range(2)]; t_h0b = [T("h0b%d" % i) for i in range(2)]
                Bz = [c1.take([128, 512], BF16) for i in range(2)]; t_Bz = [T("Bz%d" % i) for i in range(2)]
                dtAz = c1.take([128, NS, 32]); t_dtAz = T("dtAz")
                dec_s = c1.take([128, NS, 32]); t_decs = T("decs")
                hcb = [c1.take([128, NS, 30], BF16) for i in range(2)]; t_hcb = [T("hcb%d" % i) for i in range(2)]
                hs_t = c1.take([128, 24, NS, 3]); t_hs = T("hs")
                hf_t = c1.take([128, 88, NS, 2]); t_hf = T("hf")
                onew_t = c1.take([128, 16, 128]); t_onew = T("onew", 16)
                ossm_t = c1.take([128, 24, NS, 3]); t_ossm = T("ossm", 24)
                offn_t = c1.take([128, 88, NS, 2]); t_offn = T("offn", 88)
            cA = Carver(R2, R2W)
            UW = 30 + n + 2 if kind == "P" else NS * 38
            uext = [cA.take([128, UW], BF16) for i in range(2)]; t_uext = [T(pfx + "uext%d" % i) for i in range(2)]
            diag = [cA.take([128, 31, 128], BF16) for i in range(2)]; t_diag = [T(pfx + "diag%d" % i) for i in range(2)]
            cB = Carver(R2, R2W)
            EW = 4 + n if kind == "P" else NS * 11
            ext = [cB.take([128, EW]) for i in range(3)]; t_ext = [T(pfx + "ext%d" % i) for i in range(3)]
            cacc = [cB.take([128, n]) for i in range(3)]; t_cacc = [T(pfx + "cacc%d" % i) for i in range(3)]
            cC = Carver(R2, R2W)
            hprev_b = cC.take([128, D_SSM], BF16); t_hprev = T(pfx + "hprev", 4)
            xw = cC.take([128, D_SSM], BF16); t_xw = T(pfx + "xw")
            cbm = cC.take([128, 4, 128]); t_cbm = T(pfx + "cbm", 4)
            seg = [cC.take([128, 128]) for i in range(4)]; t_seg = [T(pfx + "seg%d" % i) for i in range(4)]
            Mp = [cC.take([128, 2, 128], BF16) for i in range(2)]; t_Mp = [T(pfx + "Mp%d" % i, 2) for i in range(2)]
            Ep = [cC.take([128, 128]) for i in range(2)]; t_Ep = [T(pfx + "Ep%d" % i, 2) for i in range(2)]
            toff = [cC.take([128, 128]) for i in range(2)]; t_toff = [T(pfx + "toff%d" % i) for i in range(2)]
            dtv = cC.take([128, 32]); t_dtv = T(pfx + "dtv")
            dta = cC.take([128, 32]); dte = cC.take([128, 32])
            dtA = cC.take([128, 32]); t_dtA = T(pfx + "dtA")
            acs = cC.take([128, 32]); t_acs = T(pfx + "acs")
            wst = cC.take([128, 32]); t_wst = T(pfx + "wst")
            dec_p = cC.take([128, 32]); t_dec = T(pfx + "dec")

            def v3(ap):
                return ap.rearrange("p (b t) -> p b t", t=TS)

            def rms_stats(src_chunks, r_t):
                for j in range(KC):
                    tb = j % 2
                    P.op("act", ("activation", dict(out=tmpB[tb][:, 0:n], in_=src_chunks[j], func=AF.Square)),
                         r=[r_t.parts[j]], w=[t_tmpB[tb]])
                    P.op("pe", ("matmul", dict(out=banks[6][:, 0:n], lhsT=ones_b[:], rhs=tmpB[tb][:, 0:n], start=(j == 0), stop=(j == KC - 1))),
                         r=[t_tmpB[tb], t_ones], w=[t_bank[6]])
                P.op("act", ("activation", dict(out=rstd[:, 0:n], in_=banks[6][:, 0:n], func=AF.Sqrt, scale=1.0 / D, bias=eps_t[:, 0:1])),
                     r=[t_bank[6], t_ones], w=[t_rstd])
                P.op("dve", ("reciprocal", dict(out=rstd[:, 0:n], in_=rstd[:, 0:n])), r=[t_rstd], w=[t_rstd])

            def modulate(l, Wt, shbase, j, out_ap, out_t):
                ta = j % 3
                if kind == "P":
                    P.op("dve", ("scalar_tensor_tensor", dict(out=tmpA[ta][:, 0:n], in0=xT[:, j, :], scalar=Wt[:, j, 0:1], in1=rstd[:, 0:n],
                                                              op0=ALU.mult, op1=ALU.mult)),
                         r=[t_xT.parts[j], t_rstd, t_mod[l]], w=[t_tmpA[ta]])
                    P.op("act", ("activation", dict(out=out_ap, in_=tmpA[ta][:, 0:n], func=AF.Identity, bias=modT[l][:, shbase + j, 0:1], scale=1.0)),
                         r=[t_tmpA[ta], t_mod[l]], w=[out_t])
                else:
                    P.op("dve", ("tensor_tensor", dict(out=tmpA[ta][:, 0:n], in0=xT[:, j, :], in1=rstd[:, 0:n], op=ALU.mult)),
                         r=[t_xT.parts[j], t_rstd], w=[t_tmpA[ta]])
                    P.op("dve", ("tensor_tensor", dict(out=v3(tmpA[ta][:, 0:n]), in0=v3(tmpA[ta][:, 0:n]),
                                                       in1=Wt[:, j, 1:17].unsqueeze(2).to_broadcast([128, NS, TS]), op=ALU.mult)),
                         r=[t_tmpA[ta], t_mod[l]], w=[t_tmpA[ta]])
                    P.op("dve", ("tensor_tensor", dict(out=v3(out_ap), in0=v3(tmpA[ta][:, 0:n]),
                                                       in1=modT[l][:, shbase + j, 1:17].unsqueeze(2).to_broadcast([128, NS, TS]), op=ALU.add)),
                         r=[t_tmpA[ta], t_mod[l]], w=[out_t])

            def resid_update(l, gbase, m, bank_i):
                if kind == "P":
                    P.op("dve", ("scalar_tensor_tensor", dict(out=xT[:, m, :], in0=banks[bank_i][:, 0:n], scalar=modT[l][:, gbase + m, 0:1],
                                                              in1=xT[:, m, :], op0=ALU.mult, op1=ALU.add)),
                         r=[t_bank[bank_i], t_mod[l], t_xT.parts[m]], w=[t_xT.parts[m]])
                else:
                    ta = m % 3
                    P.op("dve", ("tensor_tensor", dict(out=v3(tmpA[ta][:, 0:n]), in0=v3(banks[bank_i][:, 0:n]),
                                                       in1=modT[l][:, gbase + m, 1:17].unsqueeze(2).to_broadcast([128, NS, TS]), op=ALU.mult)),
                         r=[t_bank[bank_i], t_mod[l]], w=[t_tmpA[ta]])
                    P.op("dve", ("tensor_tensor", dict(out=xT[:, m, :], in0=xT[:, m, :], in1=tmpA[ta][:, 0:n], op=ALU.add)),
                         r=[t_tmpA[ta], t_xT.parts[m]], w=[t_xT.parts[m]])

            def hT_chunks():
                return [hT[:, kt, :] for kt in range(KC)]

            xtok_ap = BIG[:, 32:48, :].rearrange("p a b -> p (a b)")

            def xtokv(t):
                return xtok_ap[:, t * 2048:(t + 1) * 2048]

            def xtok_parts(t, c0=0, c1_=2048):
                lo = (t * 2048 + c0) // n
                hi = (t * 2048 + c1_ - 1) // n
                return [t_BIG.parts[32 + i] for i in range(lo, hi + 1)]

            def dw_load(K, ei, bank_i, halo_ap, halo_t, hist_ap, hist_t):
                H = K - 1
                e_t = t_ext[ei]
                if kind == "P":
                    P.op("act", ("copy", dict(out=ext[ei][:, H:H + n], in_=banks[bank_i][:, 0:n])), r=[t_bank[bank_i]], w=[e_t])
                    P.op("dve", ("tensor_copy", dict(out=ext[ei][:, 0:H], in_=halo_ap)), r=[halo_t], w=[e_t])
                else:
                    W = H + TS
                    ev = ext[ei][:, 0:NS * W].rearrange("p (b w) -> p b w", w=W)
                    P.op("act", ("copy", dict(out=ev[:, :, H:W], in_=v3(banks[bank_i][:, 0:n]))), r=[t_bank[bank_i]], w=[e_t])
                    P.op("dve", ("tensor_copy", dict(out=ev[:, :, 0:H], in_=hist_ap)), r=[hist_t], w=[e_t])

            def dw_conv(K, ei, wcol, bcol, ci):
                H = K - 1
                e_t = t_ext[ei]
                if kind == "P":
                    xs = lambda k: ext[ei][:, k:k + n]
                    ov = lambda ap: ap
                else:
                    W = H + TS
                    ev = ext[ei][:, 0:NS * W].rearrange("p (b w) -> p b w", w=W)
                    xs = lambda k: ev[:, :, k:k + TS]
                    ov = v3
                ca = cacc[ci]
                ct = t_cacc[ci]
                P.op("dve", ("tensor_scalar", dict(out=ov(ca[:, 0:n]), in0=xs(0), scalar1=wcol(0), scalar2=bcol, op0=ALU.mult, op1=ALU.add)),
                     r=[e_t, t_par], w=[ct])
                for k in range(1, K):
                    P.op("dve", ("scalar_tensor_tensor", dict(out=ov(ca[:, 0:n]), in0=xs(k), scalar=wcol(k), in1=ov(ca[:, 0:n]),
                                                              op0=ALU.mult, op1=ALU.add)),
                         r=[e_t, t_par, ct], w=[ct])

            def mixer(l, bi):
                rms_stats([xT[:, j, :] for j in range(KC)], t_xT)
                for j in range(KC):
                    modulate(l, Wm[l], 0, j, hT[:, j, :], t_hT.parts[j])
                rd_h = [t_hT.parts[kt] for kt in range(KC)]
                if kind == "S":
                    P.dma("sp", ("dma_start", dict(out=ohist_s[l], in_=hconf[l])), "o_fin")
                pending = [None]

                def uview(ui):
                    return uext[ui][:, 0:NS * 38].rearrange("p (b w) -> p b w", w=38)

                def conf_conv(j, ui, bq):
                    di = j % 2
                    P.op("dve", ("tensor_tensor", dict(out=diag[di], in0=ident.unsqueeze(1).to_broadcast([128, 31, 128]),
                                                       in1=pcol(l, "cw", j * 31, j * 31 + 31).unsqueeze(2).to_broadcast([128, 31, 128]), op=ALU.mult)),
                         r=[t_cst, t_par], w=[t_diag[di]])
                    if stats_pend[0] is not None:
                        stats_pend[0]()
                        stats_pend[0] = None
                    for k in range(CONF_K):
                        if kind == "P":
                            rhs = uext[ui][:, k:k + n]
                            o_ap = banks[bq][:, 0:n]
                        else:
                            rhs = uview(ui)[:, :, k:k + TS]
                            o_ap = v3(banks[bq][:, 0:n])
                        P.op("pe", ("matmul", dict(out=o_ap, lhsT=diag[di][:, k, :], rhs=rhs, start=(k == 0), stop=(k == CONF_K - 1))),
                             r=[t_diag[di], t_uext[ui]], w=[t_bank[bq]])
                    P.op("act", ("activation", dict(out=BIG[:, j, :], in_=banks[bq][:, 0:n], func=AF.Identity, bias=pcol(l, "cb", j, j + 1), scale=1.0)),
                         r=[t_bank[bq], t_par], w=[t_BIG.parts[j]])
                    tb = j % 2
                    P.op("act", ("activation", dict(out=tmpB[tb][:, 0:n], in_=banks[bq][:, 0:n], func=AF.Square, bias=pcol(l, "cb", j, j + 1), scale=1.0)),
                         r=[t_bank[bq], t_par], w=[t_tmpB[tb]])
                    stats_pend[0] = (lambda j=j, tb=tb: conf_stats(j, tb))

                def conf_stats(j, tb):
                    P.op("pe", ("matmul", dict(out=banks[6][:, 0:n], lhsT=ones_b[:], rhs=BIG[:, j, :], start=(j == 0), stop=(j == KC - 1))),
                         r=[t_BIG.parts[j], t_ones], w=[t_bank[6]])
                    P.op("pe", ("matmul", dict(out=banks[7][:, 0:n], lhsT=ones_b[:], rhs=tmpB[tb][:, 0:n], start=(j == 0), stop=(j == KC - 1))),
                         r=[t_tmpB[tb], t_ones], w=[t_bank[7]])

                stats_pend = [None]

                for j in range(KC):
                    uv = wload(w_in[l, j], cache=c_w_in[l, j])
                    ug = wload(w_in[l, 16 + j], cache=c_w_in[l, 16 + j])
                    bv, bg, bq = (0, 1, 2) if j % 2 == 0 else (3, 4, 5)
                    proj(bv, [(uv, 16)], hT_chunks(), n, rd_h)
                    proj(bg, [(ug, 16)], hT_chunks(), n, rd_h)
                    if pending[0] is not None:
                        pending[0]()
                    ui = j % 2
                    ta = j % 3
                    if kind == "P":
                        P.op("dve", ("tensor_copy", dict(out=uext[ui][:, 0:30], in_=halo_cb[l][:, j, :])),
                             r=[t_halo_cb[l].parts[j]], w=[t_uext[ui]])
                    else:
                        hi = j % 2
                        P.dma("pool", ("dma_start", dict(out=hcb[hi], in_=hconf[l, :, j])), "hcb%d" % hi, w=[t_hcb[hi]])
                        P.op("dve", ("tensor_copy", dict(out=uview(ui)[:, :, 0:30], in_=hcb[hi])),
                             r=[t_hcb[hi]], w=[t_uext[ui]])
                    P.op("act", ("activation", dict(out=tmpA[ta][:, 0:n], in_=banks[bg][:, 0:n], func=AF.Sigmoid)),
                         r=[t_bank[bg]], w=[t_tmpA[ta]])
                    if kind == "P":
                        P.op("dve", ("tensor_tensor", dict(out=uext[ui][:, 30:30 + n], in0=banks[bv][:, 0:n], in1=tmpA[ta][:, 0:n], op=ALU.mult)),
                             r=[t_bank[bv], t_tmpA[ta]], w=[t_uext[ui]])
                        P.op("dve", ("tensor_tensor", dict(out=halo_c32[l][:, j, :], in0=banks[bv][:, n - 30:n], in1=tmpA[ta][:, n - 30:n], op=ALU.mult)),
                             r=[t_bank[bv], t_tmpA[ta]], w=[t_halo_c32[l].parts[j]])
                        P.op("act", ("copy", dict(out=halo_cb[l][:, j, :], in_=halo_c32[l][:, j, :])),
                             r=[t_halo_c32[l].parts[j], t_uext[ui]], w=[t_halo_cb[l].parts[j]])
                    else:
                        P.op("dve", ("tensor_tensor", dict(out=uview(ui)[:, :, 30:38], in0=v3(banks[bv][:, 0:n]),
                                                           in1=v3(tmpA[ta][:, 0:n]), op=ALU.mult)),
                             r=[t_bank[bv], t_tmpA[ta]], w=[t_uext[ui]])
                        P.op("dve", ("tensor_tensor", dict(out=onew_t[:, j, :], in0=banks[bv][:, 0:n], in1=tmpA[ta][:, 0:n], op=ALU.mult)),
                             r=[t_bank[bv], t_tmpA[ta]], w=[t_onew.parts[j]])
                    pending[0] = (lambda j=j, ui=ui, bq=bq: conf_conv(j, ui, bq))
                pending[0]()
                stats_pend[0]()
                if kind == "S":
                    P.dma("sp", ("dma_start", dict(out=onew_s[l], in_=onew_t)), "o_onew", r=[t_onew])
                P.barrier()
                P.op("act", ("activation", dict(out=mu_t[:, 0:n], in_=banks[6][:, 0:n], func=AF.Copy, scale=1.0 / D)), r=[t_bank[6]], w=[t_mu])
                P.op("dve", ("tensor_tensor", dict(out=tmpA[0][:, 0:n], in0=mu_t[:, 0:n], in1=mu_t[:, 0:n], op=ALU.mult)), r=[t_mu], w=[t_tmpA[0]])
                P.op("dve", ("scalar_tensor_tensor", dict(out=tmpA[0][:, 0:n], in0=banks[7][:, 0:n], scalar=1.0 / D, in1=tmpA[0][:, 0:n], op0=ALU.mult, op1=ALU.subtract)),
                     r=[t_bank[7], t_tmpA[0]], w=[t_tmpA[0]])
                P.op("act", ("activation", dict(out=rstd[:, 0:n], in_=tmpA[0][:, 0:n], func=AF.Sqrt, scale=1.0, bias=eps_t[:, 1:2])), r=[t_tmpA[0], t_ones], w=[t_rstd])
                P.op("dve", ("reciprocal", dict(out=rstd[:, 0:n], in_=rstd[:, 0:n])), r=[t_rstd], w=[t_rstd])
                for j in range(KC):
                    ta = 1 + j % 2
                    P.op("dve", ("tensor_tensor", dict(out=tmpA[ta][:, 0:n], in0=BIG[:, j, :], in1=mu_t[:, 0:n], op=ALU.subtract)),
                         r=[t_BIG.parts[j], t_mu], w=[t_tmpA[ta]])
                    P.op("dve", ("tensor_tensor", dict(out=tmpA[ta][:, 0:n], in0=tmpA[ta][:, 0:n], in1=rstd[:, 0:n], op=ALU.mult)),
                         r=[t_tmpA[ta], t_rstd], w=[t_tmpA[ta]])
                    P.op("act", ("activation", dict(out=BIG[:, j, :], in_=tmpA[ta][:, 0:n], func=AF.Silu,
                                                    scale=pcol(l, "lnw", j, j + 1), bias=pcol(l, "lnb", j, j + 1))),
                         r=[t_tmpA[ta], t_par], w=[t_BIG.parts[j]])

                if kind == "S":
                    P.dma("sp", ("dma_start", dict(out=hs_t, in_=hssm[l])), "hs", w=[t_hs])
                P.dma("pool", ("dma_start", dict(out=wdt_t[:], in_=w_dt[l])), "wdt", w=[t_wdt])
                pend = [None]

                def xbc_postA(jj, ci):
                    ta = jj % 3
                    if jj < 20:
                        P.op("act", ("activation", dict(out=tmpA[ta][:, 0:n], in_=cacc[ci][:, 0:n], func=AF.Silu)), r=[t_cacc[ci]], w=[t_tmpA[ta]])
                        if jj >= 16:
                            g = jj - 16
                            P.op("dve", ("tensor_copy", dict(out=BT[:, g, :], in_=tmpA[ta][:, 0:n])), r=[t_tmpA[ta]], w=[t_BT.parts[g]])
                    else:
                        g = jj - 20
                        P.op("act", ("activation", dict(out=CT[:, g, :], in_=cacc[ci][:, 0:n], func=AF.Silu)), r=[t_cacc[ci]], w=[t_CT.parts[g]])

                def xbc_postB(jj):
                    ta = jj % 3
                    if jj >= 20:
                        return
                    g = jj - 16
                    for t in range(NTT):
                        bq = 4 + (t % 2)
                        q = (jj % 4)
                        P.op("pe", ("transpose", dict(out=banks[bq][:, q * 128:(q + 1) * 128], in_=tmpA[ta][:, t * 128:(t + 1) * 128], identity=ident)),
                             r=[t_tmpA[ta], t_cst], w=[t_bank[bq].parts[q]])
                        if jj < 16:
                            P.op("act", ("copy", dict(out=xtokv(t)[:, jj * 128:(jj + 1) * 128], in_=banks[bq][:, q * 128:(q + 1) * 128])),
                                 r=[t_bank[bq].parts[q]], w=xtok_parts(t, jj * 128, (jj + 1) * 128))
                        else:
                            P.op("act", ("copy", dict(out=Btok[:, t, g * 128:(g + 1) * 128], in_=banks[bq][:, q * 128:(q + 1) * 128])),
                                 r=[t_bank[bq].parts[q]], w=[t_Btok.parts[t]])

                for jj in range(24):
                    u = wload(w_in[l, 48 + jj], cache=c_w_in[l, 48 + jj])
                    b = jj % 4
                    proj(b, [(u, 16)], hT_chunks(), n, rd_h)
                    ei = jj % 3
                    ci = jj % 3
                    dw_load(SSM_K, ei, b, halo_s[l][:, jj, :], t_halo_s[l].parts[jj], hs_t[:, jj] if kind == "S" else None, t_hs if kind == "S" else None)
                    if kind == "P":
                        P.op("act", ("copy", dict(out=halo_s[l][:, jj, :], in_=ext[ei][:, n:n + 3])), r=[t_ext[ei]], w=[t_halo_s[l].parts[jj]])
                    else:
                        ev = ext[ei][:, 0:NS * 11].rearrange("p (b w) -> p b w", w=11)
                        P.op("act", ("copy", dict(out=ossm_t[:, jj], in_=ev[:, :, 8:11])), r=[t_ext[ei]], w=[t_ossm.parts[jj]])
                    if jj >= 2:
                        xbc_postB(jj - 2)
                    if jj >= 1:
                        xbc_postA(jj - 1, (jj - 1) % 3)
                    dw_conv(SSM_K, ei, lambda k, jj=jj: pcol(l, "scw", jj * 4 + k, jj * 4 + k + 1), pcol(l, "scb", jj, jj + 1), ci)
                xbc_postB(22)
                xbc_postA(23, 23 % 3)
                xbc_postB(23)
                if kind == "S":
                    P.dma("sp", ("dma_start", dict(out=ossm_s[l], in_=ossm_t)), "o_ossm", r=[t_ossm])
                for t in range(NTT):
                    q = t % 4
                    for kt in range(KC):
                        P.op("pe", ("matmul", dict(out=banks[7][:, q * 128:q * 128 + 32], lhsT=hT[:, kt, t * 128:(t + 1) * 128], rhs=wdt_t[:, kt, :],
                                                   start=(kt == 0), stop=(kt == KC - 1))),
                             r=[t_hT.parts[kt], t_wdt], w=[t_bank[7].parts[q]])
                    P.op("dve", ("tensor_tensor", dict(out=dtraw[t][:], in0=banks[7][:, q * 128:q * 128 + 32], in1=pcol(l, "dtb"), op=ALU.add)),
                         r=[t_bank[7].parts[q], t_par], w=[t_dtraw[t]])
                P.barrier()

                if kind == "P":
                    if bi == 0:
                        P.op("dve", ("memset", dict(ap=hstate[:], constant=0.0)), w=[t_hstate])
                    else:
                        P.dma("sp", ("dma_start", dict(out=hstate[:], in_=ostate_p[l])), "hst_in", r=[t_ost[l]], w=[t_hstate])
                    for g in range(GROUPS):
                        P.op("act", ("copy", dict(out=hprev_b[:, g * 512:(g + 1) * 512], in_=hstate[:, g * 512:(g + 1) * 512])),
                             r=[t_hstate.parts[g]], w=[t_hprev.parts[g]])
                Tm = tri if kind == "P" else btri
                Sm = ones_f[:] if kind == "P" else same
                for t in range(NTT):
                    ssd_tile(l, t, Tm, Sm)
                if kind == "P":
                    P.dma("sp", ("dma_start", dict(out=ostate_p[l], in_=hstate[:])), "hst_out", r=[t_hstate], w=[t_ost[l]])
                P.barrier()

                for j in range(KC):
                    u = wload(w_in[l, 32 + j], cache=c_w_in[l, 32 + j])
                    b = j % 6
                    proj(b, [(u, 16)], hT_chunks(), n, rd_h)
                    ta = j % 3
                    tb = j % 2
                    P.op("act", ("activation", dict(out=tmpA[ta][:, 0:n], in_=banks[b][:, 0:n], func=AF.Silu)), r=[t_bank[b]], w=[t_tmpA[ta]])
                    P.op("dve", ("tensor_tensor", dict(out=BIG[:, 16 + j, :], in0=BIG[:, 16 + j, :], in1=tmpA[ta][:, 0:n], op=ALU.mult)),
                         r=[t_tmpA[ta], t_BIG.parts[16 + j]], w=[t_BIG.parts[16 + j]])
                    P.op("act", ("activation", dict(out=tmpB[tb][:, 0:n], in_=BIG[:, 16 + j, :], func=AF.Square)), r=[t_BIG.parts[16 + j]], w=[t_tmpB[tb]])
                    P.op("pe", ("matmul", dict(out=banks[6][:, 0:n], lhsT=ones_b[:], rhs=tmpB[tb][:, 0:n], start=(j == 0), stop=(j == KC - 1))),
                         r=[t_tmpB[tb], t_ones], w=[t_bank[6]])
                P.op("act", ("activation", dict(out=rstd[:, 0:n], in_=banks[6][:, 0:n], func=AF.Sqrt, scale=1.0 / D, bias=eps_t[:, 0:1])), r=[t_bank[6], t_ones], w=[t_rstd])
                P.op("dve", ("reciprocal", dict(out=rstd[:, 0:n], in_=rstd[:, 0:n])), r=[t_rstd], w=[t_rstd])
                for j in range(KC):
                    P.op("dve", ("scalar_tensor_tensor", dict(out=BIG[:, 16 + j, :], in0=BIG[:, 16 + j, :], scalar=pcol(l, "snw", j, j + 1), in1=rstd[:, 0:n],
                                                              op0=ALU.mult, op1=ALU.mult)),
                         r=[t_BIG.parts[16 + j], t_par, t_rstd], w=[t_BIG.parts[16 + j]])
                for m in range(KC):
                    u0 = wload(w_out[l, m, :, 0:16, :], cache=c_w_out[l, m, :, 0:16, :])
                    u1 = wload(w_out[l, m, :, 16:32, :], cache=c_w_out[l, m, :, 16:32, :])
                    b = m % 6
                    proj(b, [(u0, 16), (u1, 16)], [BIG[:, kt, :] for kt in range(32)], n, [t_BIG.parts[kt] for kt in range(32)])
                    resid_update(l, 32, m, b)

            def ssd_tile(l, t, Tm, Sm):
                tcs = slice(t * 128, (t + 1) * 128)
                xt_parts = xtok_parts(t)
                P.op("act", ("activation", dict(out=dta, in_=dtraw[t][:], func=AF.Abs)), r=[t_dtraw[t]], w=[t_dtv])
                P.op("act", ("activation", dict(out=dte, in_=dta, func=AF.Exp, scale=-1.0)), r=[t_dtv], w=[t_dtv])
                P.op("act", ("activation", dict(out=dte, in_=dte, func=AF.Ln, bias=1.0, scale=1.0)), r=[t_dtv], w=[t_dtv])
                P.op("dve", ("scalar_tensor_tensor", dict(out=dtv, in0=dtraw[t][:], scalar=0.0, in1=dte, op0=ALU.max, op1=ALU.add)),
                     r=[t_dtraw[t], t_dtv], w=[t_dtv])
                P.op("dve", ("tensor_tensor", dict(out=dtA, in0=dtv, in1=a_bc[l][:], op=ALU.mult)), r=[t_dtv, t_par], w=[t_dtA])
                P.op("pe", ("matmul", dict(out=banks[7][:, 0:32], lhsT=Tm, rhs=dtA, start=True, stop=True)), r=[t_cst, t_dtA], w=[t_bank[7].parts[0]])
                P.op("pe", ("matmul", dict(out=banks[7][:, 128:160], lhsT=Sm, rhs=dtA, start=True, stop=True)), r=[t_cst, t_ones, t_dtA], w=[t_bank[7].parts[1]])
                P.op("act", ("copy", dict(out=acs, in_=banks[7][:, 0:32])), r=[t_bank[7].parts[0]], w=[t_acs])
                P.op("dve", ("tensor_tensor", dict(out=wst, in0=banks[7][:, 128:160], in1=acs, op=ALU.subtract)), r=[t_bank[7].parts[1], t_acs], w=[t_wst])
                P.op("act", ("activation", dict(out=wst, in_=wst, func=AF.Exp)), r=[t_wst], w=[t_wst])
                P.op("dve", ("tensor_tensor", dict(out=wst, in0=wst, in1=dtv, op=ALU.mult)), r=[t_wst, t_dtv], w=[t_wst])
                P.op("dve", ("tensor_tensor", dict(out=xw.rearrange("p (h d) -> p h d", d=HP), in0=xtokv(t).rearrange("p (h d) -> p h d", d=HP),
                                                   in1=wst.unsqueeze(2).to_broadcast([128, HEADS, HP]), op=ALU.mult)),
                     r=xt_parts + [t_wst], w=[t_xw])
                if kind == "P":
                    P.op("act", ("activation", dict(out=dec_p, in_=banks[7][:, 128:160], func=AF.Exp)), r=[t_bank[7].parts[1]], w=[t_dec])
                else:
                    P.op("dve", ("tensor_tensor", dict(out=dtAz, in0=dtA.unsqueeze(1).to_broadcast([128, NS, 32]),
                                                       in1=seqm.unsqueeze(2).to_broadcast([128, NS, 32]), op=ALU.mult)),
                         r=[t_dtA, t_cst], w=[t_dtAz])
                    P.op("pe", ("matmul", dict(out=banks[6][:, 0:512], lhsT=ones_f[:], rhs=dtAz.rearrange("p b h -> p (b h)"), start=True, stop=True)),
                         r=[t_ones, t_dtAz], w=[t_bank[6]])
                    P.op("act", ("activation", dict(out=dec_s.rearrange("p b h -> p (b h)"), in_=banks[6][:, 0:512], func=AF.Exp)), r=[t_bank[6]], w=[t_decs])
                for g in range(GROUPS):
                    P.op("pe", ("matmul", dict(out=banks[6][:, g * 128:(g + 1) * 128], lhsT=BT[:, g, tcs], rhs=CT[:, g, tcs], start=True, stop=True)),
                         r=[t_BT.parts[g], t_CT.parts[g]], w=[t_bank[6].parts[g]])
                    P.op("dve", ("tensor_tensor", dict(out=cbm[:, g, :], in0=banks[6][:, g * 128:(g + 1) * 128], in1=Tm, op=ALU.mult)),
                         r=[t_bank[6].parts[g], t_cst], w=[t_cbm.parts[g]])
                if kind == "P":
                    for q in range(16):
                        g = q // 4
                        P.op("pe", ("matmul", dict(out=banks[q // 4][:, (q % 4) * 128:(q % 4 + 1) * 128], lhsT=hprev_b[:, q * 128:(q + 1) * 128],
                                                   rhs=CT[:, g, tcs], start=True, stop=True)),
                             r=[t_hprev.parts[g], t_CT.parts[g]], w=[t_bank[q // 4].parts[q % 4]])
                    for g in range(GROUPS):
                        P.op("pe", ("matmul", dict(out=banks[6][:, 0:512], lhsT=Btok[:, t, g * 128:(g + 1) * 128], rhs=xw[:, g * 512:(g + 1) * 512], start=True, stop=True)),
                             r=[t_Btok.parts[t], t_xw], w=[t_bank[6]])
                        hv = hstate[:, g * 512:(g + 1) * 512]
                        hv3 = hv.rearrange("p (h d) -> p h d", d=HP)
                        P.op("dve", ("tensor_tensor", dict(out=hv3, in0=hv3,
                                                           in1=dec_p[:, g * 8:(g + 1) * 8].unsqueeze(2).to_broadcast([128, 8, HP]), op=ALU.mult)),
                             r=[t_hstate.parts[g], t_dec], w=[t_hstate.parts[g]])
                        P.op("dve", ("tensor_tensor", dict(out=hv, in0=hv, in1=banks[6][:, 0:512], op=ALU.add)),
                             r=[t_hstate.parts[g], t_bank[6]], w=[t_hstate.parts[g]])
                        P.op("act", ("copy", dict(out=hprev_b[:, g * 512:(g + 1) * 512], in_=hv)),
                             r=[t_hstate.parts[g]], w=[t_hprev.parts[g]])
                else:
                    for b in range(NS):
                        hi = b % 2
                        P.dma("sp", ("dma_start", dict(out=h0f[hi], in_=h0T[l, b])), "h0f%d" % hi, w=[t_h0f[hi]])
                        P.op("act", ("copy", dict(out=h0b[hi], in_=h0f[hi])), r=[t_h0f[hi]], w=[t_h0b[hi]])
                        P.op("dve", ("tensor_scalar", dict(out=Bz[hi], in0=Btok[:, 0, :], scalar1=seqm[:, b:b + 1], scalar2=None, op0=ALU.mult)),
                             r=[t_Btok.parts[0], t_cst], w=[t_Bz[hi]])
                        for q in range(16):
                            g = q // 4
                            c0 = (q % 4) * 128 + b * TS
                            P.op("pe", ("matmul", dict(out=banks[q // 4][:, c0:c0 + TS],
                                                       lhsT=h0b[hi][:, q * 128:(q + 1) * 128], rhs=CT[:, g, b * TS:(b + 1) * TS], start=True, stop=True)),
                                 r=[t_h0b[hi], t_CT.parts[g]], w=[t_bank[q // 4].parts[q % 4]])
                        for g in range(GROUPS):
                            P.op("pe", ("matmul", dict(out=banks[6][:, 0:512], lhsT=Bz[hi][:, g * 128:(g + 1) * 128], rhs=xw[:, g * 512:(g + 1) * 512], start=True, stop=True)),
                                 r=[t_Bz[hi], t_xw], w=[t_bank[6]])
                            hv = h0f[hi][:, g * 512:(g + 1) * 512]
                            hv3 = hv.rearrange("p (h d) -> p h d", d=HP)
                            P.op("dve", ("tensor_tensor", dict(out=hv3, in0=hv3,
                                                               in1=dec_s[:, b, g * 8:(g + 1) * 8].unsqueeze(2).to_broadcast([128, 8, HP]), op=ALU.mult)),
                                 r=[t_h0f[hi].parts[g], t_decs, t_h0b[hi]], w=[t_h0f[hi].parts[g]])
                            P.op("dve", ("tensor_tensor", dict(out=hv, in0=hv, in1=banks[6][:, 0:512], op=ALU.add)),
                                 r=[t_h0f[hi].parts[g], t_bank[6]], w=[t_h0f[hi].parts[g]])
                        P.dma("sp", ("dma_start", dict(out=ostate_s[l, b], in_=h0f[hi])), "h0o%d" % hi, r=[t_h0f[hi]])
                P.op("dve", ("tensor_tensor", dict(out=xw.rearrange("p (h d) -> p h d", d=HP), in0=xtokv(t).rearrange("p (h d) -> p h d", d=HP),
                                                   in1=pcol(l, "dsk").unsqueeze(2).to_broadcast([128, HEADS, HP]), op=ALU.mult)),
                     r=xt_parts + [t_par], w=[t_xw])
                def st1(q):
                    g = q // 4
                    pi = q % 2
                    for hl in range(2):
                        h = 2 * q + hl
                        si = h % 4
                        aq = q % 4
                        ab = 4 + hl
                        aqs = slice(aq * 128, (aq + 1) * 128)
                        P.op("pe", ("matmul", dict(out=banks[ab][:, aqs], lhsT=dtA[:, h:h + 1].to_broadcast([128, 128]), rhs=Tm, start=True, stop=True)),
                             r=[t_dtA, t_cst], w=[t_bank[ab].parts[aq]])
                    for hl in range(2):
                        h = 2 * q + hl
                        si = h % 4
                        aq = q % 4
                        ab = 4 + hl
                        aqs = slice(aq * 128, (aq + 1) * 128)
                        P.op("dve", ("tensor_scalar", dict(out=seg[si], in0=banks[ab][:, aqs], scalar1=acs[:, h:h + 1], scalar2=0.0,
                                                           op0=ALU.subtract, op1=ALU.min)),
                             r=[t_bank[ab].parts[aq], t_acs], w=[t_seg[si]])
                        P.op("act", ("activation", dict(out=Ep[pi][hl * 64:(hl + 1) * 64, :], in_=banks[ab][hl * 64:(hl + 1) * 64, aqs], func=AF.Exp)),
                             r=[t_bank[ab].parts[aq]], w=[t_Ep[pi].parts[hl]])
                        P.op("act", ("activation", dict(out=seg[si], in_=seg[si], func=AF.Exp)), r=[t_seg[si]], w=[t_seg[si]])

                def st2(q):
                    g = q // 4
                    pi = q % 2
                    yb = 6 + q % 2
                    yq = ((q // 2) % 2) * 2
                    ysl = slice(yq * 128, yq * 128 + 256)
                    for hl in range(2):
                        h = 2 * q + hl
                        si = h % 4
                        P.op("dve", ("scalar_tensor_tensor", dict(out=Mp[pi][:, hl, :], in0=seg[si], scalar=dtv[:, h:h + 1], in1=cbm[:, g, :], op0=ALU.mult, op1=ALU.mult)),
                             r=[t_seg[si], t_dtv, t_cbm.parts[g]], w=[t_Mp[pi].parts[hl]])
                    P.op("pe", ("matmul", dict(out=banks[yb][:, ysl], lhsT=xtokv(t)[:, q * 128:(q + 1) * 128],
                                               rhs=Mp[pi].rearrange("p a b -> p (a b)"), start=True, stop=False)),
                         r=xtok_parts(t, q * 128, (q + 1) * 128) + [t_Mp[pi]], w=[t_bank[yb].parts[yq], t_bank[yb].parts[yq + 1]])
                    P.op("pe", ("matmul", dict(out=banks[yb][:, ysl], lhsT=xw[:, q * 128:(q + 1) * 128],
                                               rhs=ident_b[:].rearrange("p a b -> p (a b)"), start=False, stop=True)),
                         r=[t_xw, t_ones], w=[t_bank[yb].parts[yq], t_bank[yb].parts[yq + 1]])
                    P.op("dve", ("tensor_tensor", dict(out=toff[pi], in0=banks[q // 4][:, (q % 4) * 128:(q % 4 + 1) * 128], in1=Ep[pi], op=ALU.mult)),
                         r=[t_bank[q // 4].parts[q % 4], t_Ep[pi]], w=[t_toff[pi]])

                def st3(q):
                    pi = q % 2
                    yb = 6 + q % 2
                    yq = ((q // 2) % 2) * 2
                    for hl in range(2):
                        ps = slice(hl * 64, (hl + 1) * 64)
                        P.op("dve", ("tensor_tensor", dict(out=BIG[ps, 16 + q, tcs], in0=banks[yb][ps, (yq + hl) * 128:(yq + hl + 1) * 128],
                                                           in1=toff[pi][ps, :], op=ALU.add)),
                             r=[t_bank[yb].parts[yq + hl], t_toff[pi]], w=[t_BIG.parts[16 + q]])

                for it in range(16 + 2):
                    if it < 16:
                        st1(it)
                    if 0 <= it - 1 < 16:
                        st2(it - 1)
                    if 0 <= it - 2 < 16:
                        st3(it - 2)

            def ffn(l):
                rms_stats([xT[:, j, :] for j in range(KC)], t_xT)
                for j in range(KC):
                    modulate(l, Wf[l], 48, j, hT[:, j, :], t_hT.parts[j])
                rd_h = [t_hT.parts[kt] for kt in range(KC)]
                if kind == "S":
                    P.dma("sp", ("dma_start", dict(out=hf_t, in_=hffn[l])), "hf", w=[t_hf])
                pend = [None]

                def post(j, c1_, c2_):
                    ta = j % 3
                    P.op("act", ("activation", dict(out=tmpA[ta][:, 0:n], in_=cacc[c2_][:, 0:n], func=AF.Silu)), r=[t_cacc[c2_]], w=[t_tmpA[ta]])
                    P.op("dve", ("tensor_tensor", dict(out=BIG[:, j, :], in0=cacc[c1_][:, 0:n], in1=tmpA[ta][:, 0:n], op=ALU.mult)),
                         r=[t_cacc[c1_], t_tmpA[ta]], w=[t_BIG.parts[j]])

                cnt = 0
                for j in range(FKC):
                    u1 = wload(w_up[l, j], cache=c_w_up[l, j])
                    u2 = wload(w_up[l, FKC + j], cache=c_w_up[l, FKC + j])
                    b1, b2 = (0, 1) if j % 3 == 0 else ((2, 3) if j % 3 == 1 else (4, 5))
                    proj(b1, [(u1, 16)], hT_chunks(), n, rd_h)
                    proj(b2, [(u2, 16)], hT_chunks(), n, rd_h)
                    cs = []
                    for (jj, b) in ((j, b1), (FKC + j, b2)):
                        ei = cnt % 3
                        cnt += 1
                        dw_load(FFN_K, ei, b, halo_f[l][:, jj, :], t_halo_f[l].parts[jj], hf_t[:, jj] if kind == "S" else None, t_hf if kind == "S" else None)
                        if kind == "P":
                            P.op("act", ("copy", dict(out=halo_f[l][:, jj, :], in_=ext[ei][:, n:n + 2])), r=[t_ext[ei]], w=[t_halo_f[l].parts[jj]])
                        else:
                            ev = ext[ei][:, 0:NS * 10].rearrange("p (b w) -> p b w", w=10)
                            P.op("act", ("copy", dict(out=offn_t[:, jj], in_=ev[:, :, 8:10])), r=[t_ext[ei]], w=[t_offn.parts[jj]])
                        cs.append((ei, jj))
                    if pend[0] is not None:
                        pend[0]()
                    for (ei, jj) in cs:
                        dw_conv(FFN_K, ei, lambda k, jj=jj: pcol(l, "fcw", jj * 3 + k, jj * 3 + k + 1), pcol(l, "fcb", jj, jj + 1), ei)
                    pend[0] = (lambda j=j, c1_=cs[0][0], c2_=cs[1][0]: post(j, c1_, c2_))
                pend[0]()
                if kind == "S":
                    P.dma("sp", ("dma_start", dict(out=offn_s[l], in_=offn_t)), "o_offn", r=[t_offn])
                for m in range(KC):
                    u0 = wload(w_dn[l, m, :, 0:16, :], cache=c_w_dn[l, m, :, 0:16, :])
                    u1 = wload(w_dn[l, m, :, 16:32, :], cache=c_w_dn[l, m, :, 16:32, :])
                    u2 = wload(w_dn[l, m, :, 32:44, :], kt=12, cache=c_w_dn[l, m, :, 32:44, :])
                    b = m % 6
                    proj(b, [(u0, 16), (u1, 16), (u2, 12)], [BIG[:, kt, :] for kt in range(FKC)], n, [t_BIG.parts[kt] for kt in range(FKC)])
                    resid_update(l, 80, m, b)

            def run_block(bi):
                if kind == "P":
                    P.dma("sp", ("dma_start", dict(out=xT, in_=xTp[:, :, bi * NBP:(bi + 1) * NBP])), "xin", w=[t_xT])
                else:
                    P.dma("sp", ("dma_start", dict(out=xT, in_=xTs)), "xin", w=[t_xT])
                for l in range(DEPTH):
                    if kind == "P" and bi == 0 and l == 1:
                        compute_mod(1)
                    mixer(l, bi)
                    ffn(l)
                rms_stats([xT[:, j, :] for j in range(KC)], t_xT)
                for j in range(KC):
                    P.op("dve", ("scalar_tensor_tensor", dict(out=xT[:, j, :], in0=xT[:, j, :], scalar=fnw_t[:, j:j + 1], in1=rstd[:, 0:n], op0=ALU.mult, op1=ALU.mult)),
                         r=[t_xT.parts[j], t_par, t_rstd], w=[t_xT.parts[j]])
                if kind == "P":
                    P.dma("sp", ("dma_start", dict(out=yTp[:, :, bi * NBP:(bi + 1) * NBP], in_=xT)), "xout", r=[t_xT])
                else:
                    P.dma("sp", ("dma_start", dict(out=yTs, in_=xT)), "xout", r=[t_xT])

            return run_block

        run_p = make_block_fns("P")
        for bi in range(NPB):
            run_p(bi)
            if bi == 0:
                P.barrier()
                wmode["first"] = False
        for l in range(DEPTH):
            P.dma("sp", ("dma_start", dict(out=oconf_p[l], in_=halo_c32[l][:])), "o_fin", r=[t_halo_c32[l]])
            P.dma("sp", ("dma_start", dict(out=ossm_p[l], in_=halo_s[l][:])), "o_fin", r=[t_halo_s[l]])
            P.dma("sp", ("dma_start", dict(out=offn_p[l], in_=halo_f[l][:])), "o_fin", r=[t_halo_f[l]])
        P.barrier()
        run_s = make_block_fns("S")
        run_s(0)
        fin_deps = [("d", k, v) for k, v in P.dmacnt.items() if k.startswith("o_") or k.startswith("xout") or k.startswith("h0o") or k == "hst_out"]
        P.streams["sp"].append((None, fin_deps, None))

        P.emit(es)
    return nc


def _fm(a):
    C = a.shape[-1]
    kc = C // 128
    lead = a.shape[:-1]
    b = a.reshape(lead + (kc, 128))
    nd = b.ndim
    perm = (nd - 1, nd - 2) + tuple(range(nd - 2))
    return np.ascontiguousarray(b.transpose(perm))


def _block_w(w, mt=None):
    K, N = w.shape
    return np.ascontiguousarray(w.reshape(K // 128, 128, N // 128, 128).transpose(2, 1, 0, 3))


def _consts():
    c = np.zeros((128, NCONST), np.float32)
    i = np.arange(128)
    c[:, C_ID:C_ID + 128] = np.eye(128, dtype=np.float32)
    c[:, C_TRI:C_TRI + 128] = (i[:, None] <= i[None, :]).astype(np.float32)
    sameseq = (i[:, None] // TS == i[None, :] // TS)
    c[:, C_BTRI:C_BTRI + 128] = (sameseq & (i[:, None] <= i[None, :])).astype(np.float32)
    c[:, C_SAME:C_SAME + 128] = sameseq.astype(np.float32)
    c[:, C_SEQM:C_SEQM + 16] = (i[:, None] // TS == np.arange(16)[None, :]).astype(np.float32)
    return c


_NC_CACHE = {}


def kernel(x_prompt, x_sample, c_prompt, c_sample, state_conf_conv, state_ssm_conv, state_ssm,
           state_ffn_conv, ada_w, ada_b, norm_mix_w, norm_ffn_w, w_in, conf_dw_w, conf_dw_b,
           conf_ln_w, conf_ln_b, ssm_conv_w, ssm_conv_b, dt_bias, a_log, d_skip, ssm_norm_w,
           w_out, ffn_w_up, ffn_dw_w, ffn_dw_b, ffn_w_down, final_norm_w):
    f = lambda a: np.asarray(a, dtype=np.float32)
    x_prompt, x_sample, c_prompt, c_sample = f(x_prompt), f(x_sample), f(c_prompt), f(c_sample)
    state_conf_conv, state_ssm_conv, state_ssm, state_ffn_conv = f(state_conf_conv), f(state_ssm_conv), f(state_ssm), f(state_ffn_conv)
    ada_w, ada_b, w_in, w_out, ffn_w_up, ffn_w_down = f(ada_w), f(ada_b), f(w_in), f(w_out), f(ffn_w_up), f(ffn_w_down)
    L = DEPTH

    def cols(v):
        return np.ascontiguousarray(f(v).reshape(-1, 128).T)

    par = np.zeros((L, 128, NPAR), np.float32)

    def put(l, name, arr):
        o, w = _off[name]
        par[l, :, o:o + w] = arr.reshape(128, w)

    for l in range(L):
        put(l, "nwm", cols(norm_mix_w[l]))
        put(l, "nwf", cols(norm_ffn_w[l]))
        put(l, "cw", np.ascontiguousarray(f(conf_dw_w[l]).reshape(31, 16, 128).transpose(2, 1, 0)))
        put(l, "cb", cols(conf_dw_b[l]))
        put(l, "lnw", cols(conf_ln_w[l]))
        put(l, "lnb", cols(conf_ln_b[l]))
        put(l, "scw", np.ascontiguousarray(f(ssm_conv_w[l]).reshape(4, 24, 128).transpose(2, 1, 0)))
        put(l, "scb", cols(ssm_conv_b[l]))
        put(l, "dtb", np.broadcast_to(f(dt_bias[l])[None, :], (128, 32)))
        put(l, "alog", np.broadcast_to(f(a_log[l])[None, :], (128, 32)))
        put(l, "dsk", np.broadcast_to(f(d_skip[l])[None, :], (128, 32)))
        put(l, "snw", cols(ssm_norm_w[l]))
        put(l, "fcw", np.ascontiguousarray(f(ffn_dw_w[l]).reshape(3, 88, 128).transpose(2, 1, 0)))
        put(l, "fcb", cols(ffn_dw_b[l]))
        put(l, "adab", cols(ada_b[l]))
    fnw = cols(final_norm_w)
    w_in_b = np.stack([_block_w(w_in[l][:, :9216]) for l in range(L)])
    w_dt = np.stack([np.ascontiguousarray(w_in[l][:, 9216:].reshape(16, 128, 32).transpose(1, 0, 2)) for l in range(L)])
    w_out_b = np.stack([_block_w(w_out[l]) for l in range(L)])
    w_up_b = np.stack([_block_w(ffn_w_up[l]) for l in range(L)])
    w_dn_b = np.stack([_block_w(ffn_w_down[l]) for l in range(L)])
    ada_blk = np.stack([_block_w(ada_w[l]) for l in range(L)])
    consts = _consts()

    in_maps = []
    for c in range(8):
        s = c % 4
        sl = slice(NS * c, NS * (c + 1))
        cc = np.concatenate([c_prompt[s][None], c_sample[sl]], axis=0)
        m = {
            "xTp": _fm(x_prompt[s]),
            "xTs": _fm(x_sample[sl].reshape(NS * TS, D)),
            "cT": _fm(cc),
            "consts": consts, "par": par, "fnw": fnw,
            "w_in": w_in_b, "w_dt": w_dt, "w_out": w_out_b, "w_up": w_up_b, "w_dn": w_dn_b, "ada": ada_blk,
            "hconf": np.stack([_fm(state_conf_conv[l, sl]) for l in range(L)]),
            "hssm": np.stack([_fm(state_ssm_conv[l, sl]) for l in range(L)]),
            "hffn": np.stack([_fm(state_ffn_conv[l, sl]) for l in range(L)]),
            "h0T": np.ascontiguousarray(state_ssm[:, sl].reshape(L, NS, HEADS * HP, NST).transpose(0, 1, 3, 2)),
        }
        in_maps.append(m)

    if "nc" not in _NC_CACHE:
        _NC_CACHE["nc"] = build_program()
    nc = _NC_CACHE["nc"]
    res = run_bass_kernel_spmd(nc, in_maps, core_ids=list(range(8)))
    R = res.results

    def unfm(a):
        nd = a.ndim
        perm = tuple(range(2, nd)) + (1, 0)
        b = a.transpose(perm)
        return np.ascontiguousarray(b).reshape(b.shape[:-2] + (b.shape[-2] * 128,))

    y_prompt = np.stack([unfm(R[s]["yTp"]) for s in range(4)])
    y_sample = np.concatenate([unfm(R[c]["yTs"]).reshape(NS, TS, D) for c in range(8)], axis=0)
    p_conf = np.stack([np.stack([unfm(R[s]["oconf_p"][l]) for s in range(4)]) for l in range(L)])
    p_sconv = np.stack([np.stack([unfm(R[s]["ossm_p"][l]) for s in range(4)]) for l in range(L)])
    p_ssm = np.stack([np.stack([R[s]["ostate_p"][l].T.reshape(HEADS, HP, NST) for s in range(4)]) for l in range(L)])
    p_ffn = np.stack([np.stack([unfm(R[s]["offn_p"][l]) for s in range(4)]) for l in range(L)])
    s_conf_l, s_sconv_l, s_ssm_l, s_ffn_l = [], [], [], []
    for l in range(L):
        cf, sc, ss, ff = [], [], [], []
        for c in range(8):
            hist = unfm(R[c]["ohist_s"][l])
            new = unfm(R[c]["onew_s"][l]).reshape(NS, TS, D_CONV)
            cf.append(np.concatenate([hist[:, TS:], new], axis=1))
            sc.append(unfm(R[c]["ossm_s"][l]))
            ss.append(R[c]["ostate_s"][l].transpose(0, 2, 1).reshape(NS, HEADS, HP, NST))
            ff.append(unfm(R[c]["offn_s"][l]))
        s_conf_l.append(np.concatenate(cf, axis=0))
        s_sconv_l.append(np.concatenate(sc, axis=0))
        s_ssm_l.append(np.concatenate(ss, axis=0))
        s_ffn_l.append(np.concatenate(ff, axis=0))
    outs = (y_prompt, y_sample, p_conf, p_sconv, p_ssm, p_ffn,
            np.stack(s_conf_l), np.stack(s_sconv_l), np.stack(s_ssm_l), np.stack(s_ffn_l))
    return tuple(np.ascontiguousarray(o, dtype=np.float32) for o in outs)
```

```python
import bisect
from contextlib import ExitStack

import numpy as np
import concourse.bass as bass
import concourse.mybir as mybir
from concourse.bass_utils import run_bass_kernel_spmd

F32 = mybir.dt.float32
BF16 = mybir.dt.bfloat16
ALU = mybir.AluOpType
AF = mybir.ActivationFunctionType

D = 2048
KC = 16
SEQ = 2048
NBP = 512
NPB = SEQ // NBP
NS = 16
TS = 8
DEPTH = 2
D_CONV = 2048
D_SSM = 2048
D_XBC = 3072
HEADS = 32
HP = 64
GROUPS = 4
NST = 128
D_FF = 5632
FKC = D_FF // 128
CONF_K = 31
SSM_K = 4
FFN_K = 3
EPS = 1e-6
NSLOT = 5

RAW_WINDOW = 2
SAME_ENG_SYNC = {"pe": False, "act": True, "dve": True, "pool": True, "sp": False}

_off = {}
_o = 0
for _n, _w in [("nwm", 16), ("nwf", 16), ("cw", 16 * 31), ("cb", 16), ("lnw", 16), ("lnb", 16),
               ("scw", 24 * 4), ("scb", 24), ("dtb", 32), ("alog", 32), ("dsk", 32), ("snw", 16),
               ("fcw", 88 * 3), ("fcb", 88), ("adab", 96)]:
    _off[_n] = (_o, _w)
    _o += _w
NPAR = _o
C_ID, C_TRI, C_BTRI, C_SAME, C_SEQM = 0, 128, 256, 384, 512
NCONST = 512 + 16


class T:
    __slots__ = ("name", "lw", "rd", "parts", "psum")

    def __init__(self, name, nparts=0):
        self.name = name
        self.lw = None
        self.rd = {}
        self.psum = False
        self.parts = [T("%s.%d" % (name, i)) for i in range(nparts)] if nparts else None


def psum_bank_tile(name):
    leaf = T(name + ".b")
    leaf.psum = True
    t = T(name)
    t.parts = [leaf] * 4
    return t


def _leaves(ts):
    out = []
    for t in ts:
        if t is None:
            continue
        if t.parts:
            out.extend(t.parts)
        else:
            out.append(t)
    return out


class Prog:
    def __init__(self, nc):
        self.nc = nc
        self.streams = {k: [] for k in ("pe", "act", "dve", "pool", "sp")}
        self.flag = {k: set() for k in self.streams}
        self.dmacnt = {}
        self.total_sems = set()
        self.recent = {k: [] for k in self.streams}

    def _deps(self, eng, r, w):
        deps = {}
        raw = set()

        def add(ev, is_raw):
            k = (ev[0], ev[1])
            if is_raw:
                raw.add(k)
            if ev[2] is None:
                deps[k] = None
            elif k in deps and deps[k] is None:
                pass
            elif deps.get(k, -1) < ev[2]:
                deps[k] = ev[2]

        for t in r:
            if t.lw is not None:
                add(t.lw, True)
            if t.psum:
                for k, v in t.rd.items():
                    if not (k[0] == "e" and k[1] == eng):
                        add((k[0], k[1], v), False)
        for t in w:
            if t.lw is not None:
                add(t.lw, False)
            for k, v in t.rd.items():
                add((k[0], k[1], v), False)
        out = []
        for (kind, key), v in deps.items():
            if kind == "e":
                if key == eng:
                    if not SAME_ENG_SYNC[eng] or (kind, key) not in raw or v not in self.recent[eng]:
                        continue
                self.flag[key].add(v)
            out.append((kind, key, v))
        return out

    def _register(self, ev, r, w):
        k = (ev[0], ev[1])
        for t in r:
            if ev[2] is None:
                t.rd[k] = None
            elif k in t.rd and t.rd[k] is None:
                pass
            elif t.rd.get(k, -1) < ev[2]:
                t.rd[k] = ev[2]
        for t in w:
            t.lw = ev
            t.rd = {}

    def op(self, eng, fn, r=(), w=()):
        r = _leaves(r)
        w = _leaves(w)
        idx = len(self.streams[eng])
        deps = self._deps(eng, r, w)
        self._register(("e", eng, idx), r, w)
        self.streams[eng].append((fn, deps, None))
        if fn is not None:
            self.recent[eng] = (self.recent[eng] + [idx])[-RAW_WINDOW:]

    def dma(self, q, fn, sem, r=(), w=()):
        r = _leaves(r)
        w = _leaves(w)
        deps = self._deps(q, r, w)
        if sem in self.total_sems:
            deps = [d for d in deps if not (d[0] == "d" and d[1] == sem)]
        self.dmacnt[sem] = self.dmacnt.get(sem, 0) + 16
        val = None if sem in self.total_sems else self.dmacnt[sem]
        self._register(("d", sem, val), r, w)
        self.streams[q].append((fn, deps, sem))

    def barrier(self):
        deps = []
        for eng in ("pe", "act", "dve", "pool"):
            st = self.streams[eng]
            for idx in range(len(st) - 1, -1, -1):
                fn, _, dsn = st[idx]
                if fn is not None and dsn is None:
                    deps.append(("e", eng, idx))
                    self.flag[eng].add(idx)
                    break
        for sem, cnt in self.dmacnt.items():
            deps.append(("d", sem, cnt))
        for eng in self.streams:
            self.streams[eng].append((None, list(deps), None))

    def emit(self, es):
        nc = self.nc
        esem = {k: es.enter_context(nc.semaphore("e_" + k)) for k in ("pe", "act", "dve", "pool")}
        dsem = {k: es.enter_context(nc.semaphore("d_" + k)) for k in self.dmacnt}
        flagged = {k: sorted(v) for k, v in self.flag.items()}
        block = es.enter_context(nc.Block())
        streams = self.streams
        dmacnt = self.dmacnt

        def run(engname, e):
            waited = {}
            fl = flagged[engname]
            fls = self.flag[engname]
            for idx, (fn, deps, dsn) in enumerate(streams[engname]):
                for kind, key, v in deps:
                    if kind == "e":
                        sem = esem[key]
                        val = bisect.bisect_right(flagged[key], v)
                    else:
                        sem = dsem[key]
                        val = dmacnt[key] if v is None else v
                    wk = (kind, key)
                    if waited.get(wk, 0) >= val:
                        continue
                    waited[wk] = val
                    e.wait_ge(sem, val)
                if fn is None:
                    continue
                ins = getattr(e, fn[0])(**fn[1])
                if dsn is not None:
                    ins.then_inc(dsem[dsn], 16)
                elif idx in fls:
                    ins.then_inc(esem[engname], 1)

        @block.tensor
        def _(e):
            run("pe", e)

        @block.scalar
        def _(e):
            run("act", e)

        @block.vector
        def _(e):
            run("dve", e)

        @block.gpsimd
        def _(e):
            run("pool", e)

        @block.sync
        def _(e):
            run("sp", e)


def build_program():
    nc = bass.Bass("TRN2", target_bir_lowering=False)

    def din(name, shape):
        return nc.dram_tensor(name, list(shape), F32, kind="ExternalInput").ap()

    def dout(name, shape):
        return nc.dram_tensor(name, list(shape), F32, kind="ExternalOutput").ap()

    xTp = din("xTp", [128, KC, SEQ])
    xTs = din("xTs", [128, KC, 128])
    cT = din("cT", [128, KC, 17])
    consts = din("consts", [128, NCONST])
    par = din("par", [DEPTH, 128, NPAR])
    fnw = din("fnw", [128, KC])
    w_in = din("w_in", [DEPTH, 72, 128, 16, 128])
    w_dt = din("w_dt", [DEPTH, 128, 16, 32])
    w_out = din("w_out", [DEPTH, 16, 128, 32, 128])
    w_up = din("w_up", [DEPTH, 88, 128, 16, 128])
    w_dn = din("w_dn", [DEPTH, 16, 128, 44, 128])
    ada = din("ada", [DEPTH, 96, 128, 16, 128])
    hconf = din("hconf", [DEPTH, 128, 16, NS, 30])
    hssm = din("hssm", [DEPTH, 128, 24, NS, 3])
    hffn = din("hffn", [DEPTH, 128, 88, NS, 2])
    h0T = din("h0T", [DEPTH, NS, 128, D_SSM])

    yTp = dout("yTp", [128, KC, SEQ])
    yTs = dout("yTs", [128, KC, 128])
    oconf_p = dout("oconf_p", [DEPTH, 128, 16, 30])
    ossm_p = dout("ossm_p", [DEPTH, 128, 24, 3])
    ostate_p = dout("ostate_p", [DEPTH, 128, D_SSM])
    offn_p = dout("offn_p", [DEPTH, 128, 88, 2])
    ohist_s = dout("ohist_s", [DEPTH, 128, 16, NS, 30])
    onew_s = dout("onew_s", [DEPTH, 128, 16, 128])
    ossm_s = dout("ossm_s", [DEPTH, 128, 24, NS, 3])
    ostate_s = dout("ostate_s", [DEPTH, NS, 128, D_SSM])
    offn_s = dout("offn_s", [DEPTH, 128, 88, NS, 2])

    def dcache(name, shape):
        return nc.dram_tensor(name, list(shape), BF16, kind="Internal").ap()

    c_w_in = dcache("c_w_in", [DEPTH, 72, 128, 16, 128])
    c_w_out = dcache("c_w_out", [DEPTH, 16, 128, 32, 128])
    c_w_up = dcache("c_w_up", [DEPTH, 88, 128, 16, 128])
    c_w_dn = dcache("c_w_dn", [DEPTH, 16, 128, 44, 128])

    P = Prog(nc)
    P.total_sems.add("par")
    P.total_sems.add("o_fin")

    with ExitStack() as es:
        def sb(name, shape, dt=F32):
            return es.enter_context(nc.sbuf_tensor("s_" + name, list(shape), dt))

        cst = sb("cst", [128, NCONST]); t_cst = T("cst")
        ident = cst[:, C_ID:C_ID + 128]
        tri = cst[:, C_TRI:C_TRI + 128]
        btri = cst[:, C_BTRI:C_BTRI + 128]
        same = cst[:, C_SAME:C_SAME + 128]
        seqm = cst[:, C_SEQM:C_SEQM + 16]
        ones_f = sb("ones_f", [128, 128]); t_ones = T("ones")
        ones_b = sb("ones_b", [128, 128], BF16)
        eps_t = sb("eps_t", [128, 2])
        ident_b = sb("ident_b", [128, 2, 128], BF16)
        part = [sb("par%d" % l, [128, NPAR]) for l in range(DEPTH)]; t_par = T("par")
        fnw_t = sb("fnw", [128, KC])
        a_bc = [sb("a_bc%d" % l, [128, 32]) for l in range(DEPTH)]
        modT = [sb("modT%d" % l, [128, 96, 17]) for l in range(DEPTH)]
        t_mod = [T("mod%d" % l) for l in range(DEPTH)]
        Wm = [sb("Wm%d" % l, [128, 16, 17]) for l in range(DEPTH)]
        Wf = [sb("Wf%d" % l, [128, 16, 17]) for l in range(DEPTH)]

        def pcol(l, name, a=0, b=None):
            o, w = _off[name]
            b = w if b is None else b
            return part[l][:, o + a:o + b]

        halo_cb = [sb("halo_cb%d" % l, [128, 16, 30], BF16) for l in range(DEPTH)]
        halo_c32 = [sb("halo_c32%d" % l, [128, 16, 30]) for l in range(DEPTH)]
        halo_s = [sb("halo_s%d" % l, [128, 24, 3]) for l in range(DEPTH)]
        halo_f = [sb("halo_f%d" % l, [128, 88, 2]) for l in range(DEPTH)]
        hstate = sb("hstate", [128, D_SSM])
        t_halo_cb = [T("halo_cb%d" % l, 16) for l in range(DEPTH)]
        t_halo_c32 = [T("halo_c32%d" % l, 16) for l in range(DEPTH)]
        t_halo_s = [T("halo_s%d" % l, 24) for l in range(DEPTH)]
        t_halo_f = [T("halo_f%d" % l, 88) for l in range(DEPTH)]
        t_hstate = T("hstate", 4)
        t_ost = [T("ost%d" % l) for l in range(DEPTH)]

        ring = sb("ring", [128, NSLOT, 16, 128], BF16); t_ring = [T("ring%d" % i) for i in range(NSLOT)]
        wdt_t = sb("wdt", [128, 16, 32], BF16); t_wdt = T("wdt")
        rstd = sb("rstd", [128, NBP]); t_rstd = T("rstd")
        mu_t = sb("mu", [128, NBP]); t_mu = T("mu")
        tmpA = [sb("tmpA%d" % i, [128, NBP]) for i in range(3)]; t_tmpA = [T("tmpA%d" % i) for i in range(3)]
        tmpB = [sb("tmpB%d" % i, [128, NBP], BF16) for i in range(2)]; t_tmpB = [T("tmpB%d" % i) for i in range(2)]
        dtraw = [sb("dtraw%d" % i, [128, 32]) for i in range(NBP // 128)]; t_dtraw = [T("dtraw%d" % i) for i in range(NBP // 128)]

        R1W = 27648
        R2W = 4608
        R1 = sb("R1", [128, R1W])
        R2 = sb("R2", [128, R2W])

        class Carver:
            def __init__(self, base, words):
                self.base = base
                self.words = words
                self.off = 0

            def take(self, shape, dt=F32):
                n = 1
                for d_ in shape[1:]:
                    n *= d_
                words = n if dt == F32 else (n + 1) // 2
                assert self.off + words <= self.words, ("arena overflow", self.off, words, self.words)
                v = self.base[:, self.off:self.off + words]
                self.off += words
                if dt != F32:
                    v = v.bitcast(dt)[:, 0:n]
                if len(shape) == 3:
                    v = v.rearrange("p (a b) -> p a b", a=shape[1])
                elif len(shape) == 4:
                    v = v.rearrange("p (a b c) -> p a b c", a=shape[1], b=shape[2])
                return v

        banks = [es.enter_context(nc.psum_tensor("bank%d" % i, [128, 512], F32)) for i in range(8)]
        t_bank = [psum_bank_tile("bank%d" % i) for i in range(8)]

        wctr = [0]

        wmode = {"first": True}

        def wload(src_ap, kt=16, cache=None):
            i = wctr[0] % NSLOT
            wctr[0] += 1
            if cache is None or wmode["first"]:
                P.dma("pool", ("dma_start", dict(out=ring[:, i, 0:kt, :], in_=src_ap)), "ring%d" % i, w=[t_ring[i]])
                if cache is not None:
                    P.dma("sp", ("dma_start", dict(out=cache, in_=ring[:, i, 0:kt, :])), "rc%d" % i, r=[t_ring[i]])
            else:
                P.dma("pool", ("dma_start", dict(out=ring[:, i, 0:kt, :], in_=cache)), "ring%d" % i, w=[t_ring[i]])
            return i

        def proj(bank_i, units, rhs_list, n, rd):
            tot = len(rhs_list)
            k = 0
            per_k = (len(rd) == tot)
            for (u, kts) in units:
                for kt in range(kts):
                    rhs = rhs_list[k]
                    P.op("pe", ("matmul", dict(out=banks[bank_i][:, 0:n], lhsT=ring[:, u, kt, :], rhs=rhs, start=(k == 0), stop=(k == tot - 1))),
                         r=[t_ring[u]] + ([rd[k]] if per_k else rd), w=[t_bank[bank_i]])
                    k += 1

        cTt = sb("cTt", [128, KC, 17])
        scT = sb("scT", [128, KC, 17], BF16)
        t_scT = T("scT")
        P.dma("sp", ("dma_start", dict(out=cst[:], in_=consts)), "par", w=[t_cst])
        for l in range(DEPTH):
            P.dma("sp", ("dma_start", dict(out=part[l][:], in_=par[l])), "par", w=[t_par])
        P.dma("sp", ("dma_start", dict(out=fnw_t[:], in_=fnw)), "par", w=[t_par])
        P.dma("sp", ("dma_start", dict(out=cTt[:], in_=cT)), "par", w=[t_scT])
        P.op("dve", ("memset", dict(ap=ones_f[:], constant=1.0)), w=[t_ones])
        P.op("dve", ("memset", dict(ap=ones_b[:], constant=1.0)), w=[t_ones])
        P.op("dve", ("tensor_copy", dict(out=ident_b[:, 0, :], in_=ident)), r=[t_cst], w=[t_ones])
        P.op("dve", ("tensor_copy", dict(out=ident_b[:, 1, :], in_=ident)), r=[t_cst], w=[t_ones])
        P.op("dve", ("memset", dict(ap=eps_t[:, 0:1], constant=EPS)), w=[t_ones])
        P.op("dve", ("memset", dict(ap=eps_t[:, 1:2], constant=1e-5)), w=[t_ones])
        for l in range(DEPTH):
            P.op("dve", ("memset", dict(ap=halo_cb[l][:], constant=0.0)), w=[t_halo_cb[l]])
            P.op("dve", ("memset", dict(ap=halo_c32[l][:], constant=0.0)), w=[t_halo_c32[l]])
            P.op("dve", ("memset", dict(ap=halo_s[l][:], constant=0.0)), w=[t_halo_s[l]])
            P.op("dve", ("memset", dict(ap=halo_f[l][:], constant=0.0)), w=[t_halo_f[l]])
            P.op("act", ("activation", dict(out=a_bc[l][:], in_=pcol(l, "alog"), func=AF.Exp)), r=[t_par], w=[t_par])
            P.op("dve", ("tensor_scalar", dict(out=a_bc[l][:], in0=a_bc[l][:], scalar1=-1.0, scalar2=None, op0=ALU.mult)), r=[t_par], w=[t_par])
        P.op("act", ("activation", dict(out=scT[:], in_=cTt[:], func=AF.Silu)), r=[t_scT], w=[t_scT])
        def compute_mod(l):
            for m in range(96):
                u = wload(ada[l, m])
                b = m % 6
                proj(b, [(u, 16)], [scT[:, kt, :] for kt in range(16)], 17, [t_scT])
                P.op("act", ("activation", dict(out=modT[l][:, m, :], in_=banks[b][:, 0:17], func=AF.Identity,
                                                bias=pcol(l, "adab", m, m + 1), scale=1.0)),
                     r=[t_bank[b], t_par], w=[t_mod[l]])
            for (Wt, base, nm) in ((Wm[l], 16, "nwm"), (Wf[l], 64, "nwf")):
                P.op("dve", ("tensor_scalar", dict(out=Wt[:], in0=modT[l][:, base:base + 16, :], scalar1=1.0, scalar2=None, op0=ALU.add)),
                     r=[t_mod[l]], w=[t_mod[l]])
                P.op("dve", ("tensor_tensor", dict(out=Wt[:], in0=Wt[:], in1=pcol(l, nm).unsqueeze(2).to_broadcast([128, 16, 17]), op=ALU.mult)),
                     r=[t_mod[l], t_par], w=[t_mod[l]])
        compute_mod(0)
        P.barrier()

        def make_block_fns(kind):
            n = NBP if kind == "P" else 128
            NTT = n // 128
            pfx = kind
            c1 = Carver(R1, R1W)
            xT = c1.take([128, KC, n]); t_xT = T(pfx + "xT", KC)
            hT = c1.take([128, KC, n], BF16); t_hT = T(pfx + "hT", KC)
            BIG = c1.take([128, 48, n], BF16); t_BIG = T(pfx + "BIG", 48)
            BT = c1.take([128, 4, n], BF16); t_BT = T(pfx + "BT", 4)
            CT = c1.take([128, 4, n], BF16); t_CT = T(pfx + "CT", 4)
            Btok = c1.take([128, NTT, 512], BF16); t_Btok = T(pfx + "Btok", NTT)
            if kind == "S":
                h0f = [c1.take([128, D_SSM]) for i in range(2)]; t_h0f = [T("h0f%d" % i, 4) for i in range(2)]
                h0b = [c1.take([128, D_SSM], BF16) for i in range(2)]; t_h0b = [T("h0b%d" % i) for i in range(2)]
                Bz = [c1.take([128, 512], BF16) for i in range(2)]; t_Bz = [T("Bz%d" % i) for i in range(2)]
                dtAz = c1.take([128, NS, 32]); t_dtAz = T("dtAz")
                dec_s = c1.take([128, NS, 32]); t_decs = T("decs")
                hcb = [c1.take([128, NS, 30], BF16) for i in range(2)]; t_hcb = [T("hcb%d" % i) for i in range(2)]
                hs_t = c1.take([128, 24, NS, 3]); t_hs = T("hs")
                hf_t = c1.take([128, 88, NS, 2]); t_hf = T("hf")
                onew_t = c1.take([128, 16, 128]); t_onew = T("onew", 16)
                ossm_t = c1.take([128, 24, NS, 3]); t_ossm = T("ossm", 24)
                offn_t = c1.take([128, 88, NS, 2]); t_offn = T("offn", 88)
            cA = Carver(R2, R2W)
            UW = 30 + n + 2 if kind == "P" else NS * 38
            uext = [cA.take([128, UW], BF16) for i in range(2)]; t_uext = [T(pfx + "uext%d" % i) for i in range(2)]
            diag = [cA.take([128, 31, 128], BF16) for i in range(2)]; t_diag = [T(pfx + "diag%d" % i) for i in range(2)]
            cB = Carver(R2, R2W)
            EW = 4 + n if kind == "P" else NS * 11
            ext = [cB.take([128, EW]) for i in range(3)]; t_ext = [T(pfx + "ext%d" % i) for i in range(3)]
            cacc = [cB.take([128, n]) for i in range(3)]; t_cacc = [T(pfx + "cacc%d" % i) for i in range(3)]
            cC = Carver(R2, R2W)
            hprev_b = cC.take([128, D_SSM], BF16); t_hprev = T(pfx + "hprev", 4)
            xw = cC.take([128, D_SSM], BF16); t_xw = T(pfx + "xw")
            cbm = cC.take([128, 4, 128]); t_cbm = T(pfx + "cbm", 4)
            seg = [cC.take([128, 128]) for i in range(4)]; t_seg = [T(pfx + "seg%d" % i) for i in range(4)]
            Mp = [cC.take([128, 2, 128], BF16) for i in range(2)]; t_Mp = [T(pfx + "Mp%d" % i, 2) for i in range(2)]
            Ep = [cC.take([128, 128]) for i in range(2)]; t_Ep = [T(pfx + "Ep%d" % i, 2) for i in range(2)]
            toff = [cC.take([128, 128]) for i in range(2)]; t_toff = [T(pfx + "toff%d" % i) for i in range(2)]
            dtv = cC.take([128, 32]); t_dtv = T(pfx + "dtv")
            dta = cC.take([128, 32]); dte = cC.take([128, 32])
            dtA = cC.take([128, 32]); t_dtA = T(pfx + "dtA")
            acs = cC.take([128, 32]); t_acs = T(pfx + "acs")
            wst = cC.take([128, 32]); t_wst = T(pfx + "wst")
            dec_p = cC.take([128, 32]); t_dec = T(pfx + "dec")

            def v3(ap):
                return ap.rearrange("p (b t) -> p b t", t=TS)

            def rms_stats(src_chunks, r_t):
                for j in range(KC):
                    tb = j % 2
                    P.op("act", ("activation", dict(out=tmpB[tb][:, 0:n], in_=src_chunks[j], func=AF.Square)),
                         r=[r_t.parts[j]], w=[t_tmpB[tb]])
                    P.op("pe", ("matmul", dict(out=banks[6][:, 0:n], lhsT=ones_b[:], rhs=tmpB[tb][:, 0:n], start=(j == 0), stop=(j == KC - 1))),
                         r=[t_tmpB[tb], t_ones], w=[t_bank[6]])
                P.op("act", ("activation", dict(out=rstd[:, 0:n], in_=banks[6][:, 0:n], func=AF.Sqrt, scale=1.0 / D, bias=eps_t[:, 0:1])),
                     r=[t_bank[6], t_ones], w=[t_rstd])
                P.op("dve", ("reciprocal", dict(out=rstd[:, 0:n], in_=rstd[:, 0:n])), r=[t_rstd], w=[t_rstd])

            def modulate(l, Wt, shbase, j, out_ap, out_t):
                ta = j % 3
                if kind == "P":
                    P.op("dve", ("scalar_tensor_tensor", dict(out=tmpA[ta][:, 0:n], in0=xT[:, j, :], scalar=Wt[:, j, 0:1], in1=rstd[:, 0:n],
                                                              op0=ALU.mult, op1=ALU.mult)),
                         r=[t_xT.parts[j], t_rstd, t_mod[l]], w=[t_tmpA[ta]])
                    P.op("act", ("activation", dict(out=out_ap, in_=tmpA[ta][:, 0:n], func=AF.Identity, bias=modT[l][:, shbase + j, 0:1], scale=1.0)),
                         r=[t_tmpA[ta], t_mod[l]], w=[out_t])
                else:
                    P.op("dve", ("tensor_tensor", dict(out=tmpA[ta][:, 0:n], in0=xT[:, j, :], in1=rstd[:, 0:n], op=ALU.mult)),
                         r=[t_xT.parts[j], t_rstd], w=[t_tmpA[ta]])
                    P.op("dve", ("tensor_tensor", dict(out=v3(tmpA[ta][:, 0:n]), in0=v3(tmpA[ta][:, 0:n]),
                                                       in1=Wt[:, j, 1:17].unsqueeze(2).to_broadcast([128, NS, TS]), op=ALU.mult)),
                         r=[t_tmpA[ta], t_mod[l]], w=[t_tmpA[ta]])
                    P.op("dve", ("tensor_tensor", dict(out=v3(out_ap), in0=v3(tmpA[ta][:, 0:n]),
                                                       in1=modT[l][:, shbase + j, 1:17].unsqueeze(2).to_broadcast([128, NS, TS]), op=ALU.add)),
                         r=[t_tmpA[ta], t_mod[l]], w=[out_t])

            def resid_update(l, gbase, m, bank_i):
                if kind == "P":
                    P.op("dve", ("scalar_tensor_tensor", dict(out=xT[:, m, :], in0=banks[bank_i][:, 0:n], scalar=modT[l][:, gbase + m, 0:1],
                                                              in1=xT[:, m, :], op0=ALU.mult, op1=ALU.add)),
                         r=[t_bank[bank_i], t_mod[l], t_xT.parts[m]], w=[t_xT.parts[m]])
                else:
                    ta = m % 3
                    P.op("dve", ("tensor_tensor", dict(out=v3(tmpA[ta][:, 0:n]), in0=v3(banks[bank_i][:, 0:n]),
                                                       in1=modT[l][:, gbase + m, 1:17].unsqueeze(2).to_broadcast([128, NS, TS]), op=ALU.mult)),
                         r=[t_bank[bank_i], t_mod[l]], w=[t_tmpA[ta]])
                    P.op("dve", ("tensor_tensor", dict(out=xT[:, m, :], in0=xT[:, m, :], in1=tmpA[ta][:, 0:n], op=ALU.add)),
                         r=[t_tmpA[ta], t_xT.parts[m]], w=[t_xT.parts[m]])

            def hT_chunks():
                return [hT[:, kt, :] for kt in range(KC)]

            xtok_ap = BIG[:, 32:48, :].rearrange("p a b -> p (a b)")

            def xtokv(t):
                return xtok_ap[:, t * 2048:(t + 1) * 2048]

            def xtok_parts(t, c0=0, c1_=2048):
                lo = (t * 2048 + c0) // n
                hi = (t * 2048 + c1_ - 1) // n
                return [t_BIG.parts[32 + i] for i in range(lo, hi + 1)]

            def dw_load(K, ei, bank_i, halo_ap, halo_t, hist_ap, hist_t):
                H = K - 1
                e_t = t_ext[ei]
                if kind == "P":
                    P.op("act", ("copy", dict(out=ext[ei][:, H:H + n], in_=banks[bank_i][:, 0:n])), r=[t_bank[bank_i]], w=[e_t])
                    P.op("dve", ("tensor_copy", dict(out=ext[ei][:, 0:H], in_=halo_ap)), r=[halo_t], w=[e_t])
                else:
                    W = H + TS
                    ev = ext[ei][:, 0:NS * W].rearrange("p (b w) -> p b w", w=W)
                    P.op("act", ("copy", dict(out=ev[:, :, H:W], in_=v3(banks[bank_i][:, 0:n]))), r=[t_bank[bank_i]], w=[e_t])
                    P.op("dve", ("tensor_copy", dict(out=ev[:, :, 0:H], in_=hist_ap)), r=[hist_t], w=[e_t])

            def dw_conv(K, ei, wcol, bcol, ci):
                H = K - 1
                e_t = t_ext[ei]
                if kind == "P":
                    xs = lambda k: ext[ei][:, k:k + n]
                    ov = lambda ap: ap
                else:
                    W = H + TS
                    ev = ext[ei][:, 0:NS * W].rearrange("p (b w) -> p b w", w=W)
                    xs = lambda k: ev[:, :, k:k + TS]
                    ov = v3
                ca = cacc[ci]
                ct = t_cacc[ci]
                P.op("dve", ("tensor_scalar", dict(out=ov(ca[:, 0:n]), in0=xs(0), scalar1=wcol(0), scalar2=bcol, op0=ALU.mult, op1=ALU.add)),
                     r=[e_t, t_par], w=[ct])
                for k in range(1, K):
                    P.op("dve", ("scalar_tensor_tensor", dict(out=ov(ca[:, 0:n]), in0=xs(k), scalar=wcol(k), in1=ov(ca[:, 0:n]),
                                                              op0=ALU.mult, op1=ALU.add)),
                         r=[e_t, t_par, ct], w=[ct])

            def mixer(l, bi):
                rms_stats([xT[:, j, :] for j in range(KC)], t_xT)
                for j in range(KC):
                    modulate(l, Wm[l], 0, j, hT[:, j, :], t_hT.parts[j])
                rd_h = [t_hT.parts[kt] for kt in range(KC)]
                if kind == "S":
                    P.dma("sp", ("dma_start", dict(out=ohist_s[l], in_=hconf[l])), "o_fin")
                pending = [None]

                def uview(ui):
                    return uext[ui][:, 0:NS * 38].rearrange("p (b w) -> p b w", w=38)

                def conf_conv(j, ui, bq):
                    di = j % 2
                    P.op("dve", ("tensor_tensor", dict(out=diag[di], in0=ident.unsqueeze(1).to_broadcast([128, 31, 128]),
                                                       in1=pcol(l, "cw", j * 31, j * 31 + 31).unsqueeze(2).to_broadcast([128, 31, 128]), op=ALU.mult)),
                         r=[t_cst, t_par], w=[t_diag[di]])
                    if stats_pend[0] is not None:
                        stats_pend[0]()
                        stats_pend[0] = None
                    for k in range(CONF_K):
                        if kind == "P":
                            rhs = uext[ui][:, k:k + n]
                            o_ap = banks[bq][:, 0:n]
                        else:
                            rhs = uview(ui)[:, :, k:k + TS]
                            o_ap = v3(banks[bq][:, 0:n])
                        P.op("pe", ("matmul", dict(out=o_ap, lhsT=diag[di][:, k, :], rhs=rhs, start=(k == 0), stop=(k == CONF_K - 1))),
                             r=[t_diag[di], t_uext[ui]], w=[t_bank[bq]])
                    P.op("act", ("activation", dict(out=BIG[:, j, :], in_=banks[bq][:, 0:n], func=AF.Identity, bias=pcol(l, "cb", j, j + 1), scale=1.0)),
                         r=[t_bank[bq], t_par], w=[t_BIG.parts[j]])
                    tb = j % 2
                    P.op("act", ("activation", dict(out=tmpB[tb][:, 0:n], in_=banks[bq][:, 0:n], func=AF.Square, bias=pcol(l, "cb", j, j + 1), scale=1.0)),
                         r=[t_bank[bq], t_par], w=[t_tmpB[tb]])
                    stats_pend[0] = (lambda j=j, tb=tb: conf_stats(j, tb))

                def conf_stats(j, tb):
                    P.op("pe", ("matmul", dict(out=banks[6][:, 0:n], lhsT=ones_b[:], rhs=BIG[:, j, :], start=(j == 0), stop=(j == KC - 1))),
                         r=[t_BIG.parts[j], t_ones], w=[t_bank[6]])
                    P.op("pe", ("matmul", dict(out=banks[7][:, 0:n], lhsT=ones_b[:], rhs=tmpB[tb][:, 0:n], start=(j == 0), stop=(j == KC - 1))),
                         r=[t_tmpB[tb], t_ones], w=[t_bank[7]])

                stats_pend = [None]

                for j in range(KC):
                    uv = wload(w_in[l, j], cache=c_w_in[l, j])
                    ug = wload(w_in[l, 16 + j], cache=c_w_in[l, 16 + j])
                    bv, bg, bq = (0, 1, 2) if j % 2 == 0 else (3, 4, 5)
                    proj(bv, [(uv, 16)], hT_chunks(), n, rd_h)
                    proj(bg, [(ug, 16)], hT_chunks(), n, rd_h)
                    if pending[0] is not None:
                        pending[0]()
                    ui = j % 2
                    ta = j % 3
                    if kind == "P":
                        P.op("dve", ("tensor_copy", dict(out=uext[ui][:, 0:30], in_=halo_cb[l][:, j, :])),
                             r=[t_halo_cb[l].parts[j]], w=[t_uext[ui]])
                    else:
                        hi = j % 2
                        P.dma("pool", ("dma_start", dict(out=hcb[hi], in_=hconf[l, :, j])), "hcb%d" % hi, w=[t_hcb[hi]])
                        P.op("dve", ("tensor_copy", dict(out=uview(ui)[:, :, 0:30], in_=hcb[hi])),
                             r=[t_hcb[hi]], w=[t_uext[ui]])
                    P.op("act", ("activation", dict(out=tmpA[ta][:, 0:n], in_=banks[bg][:, 0:n], func=AF.Sigmoid)),
                         r=[t_bank[bg]], w=[t_tmpA[ta]])
                    if kind == "P":
                        P.op("dve", ("tensor_tensor", dict(out=uext[ui][:, 30:30 + n], in0=banks[bv][:, 0:n], in1=tmpA[ta][:, 0:n], op=ALU.mult)),
                             r=[t_bank[bv], t_tmpA[ta]], w=[t_uext[ui]])
                        P.op("dve", ("tensor_tensor", dict(out=halo_c32[l][:, j, :], in0=banks[bv][:, n - 30:n], in1=tmpA[ta][:, n - 30:n], op=ALU.mult)),
                             r=[t_bank[bv], t_tmpA[ta]], w=[t_halo_c32[l].parts[j]])
                        P.op("act", ("copy", dict(out=halo_cb[l][:, j, :], in_=halo_c32[l][:, j, :])),
                             r=[t_halo_c32[l].parts[j], t_uext[ui]], w=[t_halo_cb[l].parts[j]])
                    else:
                        P.op("dve", ("tensor_tensor", dict(out=uview(ui)[:, :, 30:38], in0=v3(banks[bv][:, 0:n]),
                                                           in1=v3(tmpA[ta][:, 0:n]), op=ALU.mult)),
                             r=[t_bank[bv], t_tmpA[ta]], w=[t_uext[ui]])
                        P.op("dve", ("tensor_tensor", dict(out=onew_t[:, j, :], in0=banks[bv][:, 0:n], in1=tmpA[ta][:, 0:n], op=ALU.mult)),
                             r=[t_bank[bv], t_tmpA[ta]], w=[t_onew.parts[j]])
                    pending[0] = (lambda j=j, ui=ui, bq=bq: conf_conv(j, ui, bq))
                pending[0]()
                stats_pend[0]()
                if kind == "S":
                    P.dma("sp", ("dma_start", dict(out=onew_s[l], in_=onew_t)), "o_onew", r=[t_onew])
                P.barrier()
                P.op("act", ("activation", dict(out=mu_t[:, 0:n], in_=banks[6][:, 0:n], func=AF.Copy, scale=1.0 / D)), r=[t_bank[6]], w=[t_mu])
                P.op("dve", ("tensor_tensor", dict(out=tmpA[0][:, 0:n], in0=mu_t[:, 0:n], in1=mu_t[:, 0:n], op=ALU.mult)), r=[t_mu], w=[t_tmpA[0]])
                P.op("dve", ("scalar_tensor_tensor", dict(out=tmpA[0][:, 0:n], in0=banks[7][:, 0:n], scalar=1.0 / D, in1=tmpA[0][:, 0:n], op0=ALU.mult, op1=ALU.subtract)),
                     r=[t_bank[7], t_tmpA[0]], w=[t_tmpA[0]])
                P.op("act", ("activation", dict(out=rstd[:, 0:n], in_=tmpA[0][:, 0:n], func=AF.Sqrt, scale=1.0, bias=eps_t[:, 1:2])), r=[t_tmpA[0], t_ones], w=[t_rstd])
                P.op("dve", ("reciprocal", dict(out=rstd[:, 0:n], in_=rstd[:, 0:n])), r=[t_rstd], w=[t_rstd])
                for j in range(KC):
                    ta = 1 + j % 2
                    P.op("dve", ("tensor_tensor", dict(out=tmpA[ta][:, 0:n], in0=BIG[:, j, :], in1=mu_t[:, 0:n], op=ALU.subtract)),
                         r=[t_BIG.parts[j], t_mu], w=[t_tmpA[ta]])
                    P.op("dve", ("tensor_tensor", dict(out=tmpA[ta][:, 0:n], in0=tmpA[ta][:, 0:n], in1=rstd[:, 0:n], op=ALU.mult)),
                         r=[t_tmpA[ta], t_rstd], w=[t_tmpA[ta]])
                    P.op("act", ("activation", dict(out=BIG[:, j, :], in_=tmpA[ta][:, 0:n], func=AF.Silu,
                                                    scale=pcol(l, "lnw", j, j + 1), bias=pcol(l, "lnb", j, j + 1))),
                         r=[t_tmpA[ta], t_par], w=[t_BIG.parts[j]])

                if kind == "S":
                    P.dma("sp", ("dma_start", dict(out=hs_t, in_=hssm[l])), "hs", w=[t_hs])
                P.dma("pool", ("dma_start", dict(out=wdt_t[:], in_=w_dt[l])), "wdt", w=[t_wdt])
                pend = [None]

                def xbc_postA(jj, ci):
                    ta = jj % 3
                    if jj < 20:
                        P.op("act", ("activation", dict(out=tmpA[ta][:, 0:n], in_=cacc[ci][:, 0:n], func=AF.Silu)), r=[t_cacc[ci]], w=[t_tmpA[ta]])
                        if jj >= 16:
                            g = jj - 16
                            P.op("dve", ("tensor_copy", dict(out=BT[:, g, :], in_=tmpA[ta][:, 0:n])), r=[t_tmpA[ta]], w=[t_BT.parts[g]])
                    else:
                        g = jj - 20
                        P.op("act", ("activation", dict(out=CT[:, g, :], in_=cacc[ci][:, 0:n], func=AF.Silu)), r=[t_cacc[ci]], w=[t_CT.parts[g]])

                def xbc_postB(jj):
                    ta = jj % 3
                    if jj >= 20:
                        return
                    g = jj - 16
                    for t in range(NTT):
                        bq = 4 + (t % 2)
                        q = (jj % 4)
                        P.op("pe", ("transpose", dict(out=banks[bq][:, q * 128:(q + 1) * 128], in_=tmpA[ta][:, t * 128:(t + 1) * 128], identity=ident)),
                             r=[t_tmpA[ta], t_cst], w=[t_bank[bq].parts[q]])
                        if jj < 16:
                            P.op("act", ("copy", dict(out=xtokv(t)[:, jj * 128:(jj + 1) * 128], in_=banks[bq][:, q * 128:(q + 1) * 128])),
                                 r=[t_bank[bq].parts[q]], w=xtok_parts(t, jj * 128, (jj + 1) * 128))
                        else:
                            P.op("act", ("copy", dict(out=Btok[:, t, g * 128:(g + 1) * 128], in_=banks[bq][:, q * 128:(q + 1) * 128])),
                                 r=[t_bank[bq].parts[q]], w=[t_Btok.parts[t]])

                for jj in range(24):
                    u = wload(w_in[l, 48 + jj], cache=c_w_in[l, 48 + jj])
                    b = jj % 4
                    proj(b, [(u, 16)], hT_chunks(), n, rd_h)
                    ei = jj % 3
                    ci = jj % 3
                    dw_load(SSM_K, ei, b, halo_s[l][:, jj, :], t_halo_s[l].parts[jj], hs_t[:, jj] if kind == "S" else None, t_hs if kind == "S" else None)
                    if kind == "P":
                        P.op("act", ("copy", dict(out=halo_s[l][:, jj, :], in_=ext[ei][:, n:n + 3])), r=[t_ext[ei]], w=[t_halo_s[l].parts[jj]])
                    else:
                        ev = ext[ei][:, 0:NS * 11].rearrange("p (b w) -> p b w", w=11)
                        P.op("act", ("copy", dict(out=ossm_t[:, jj], in_=ev[:, :, 8:11])), r=[t_ext[ei]], w=[t_ossm.parts[jj]])
                    if jj >= 2:
                        xbc_postB(jj - 2)
                    if jj >= 1:
                        xbc_postA(jj - 1, (jj - 1) % 3)
                    dw_conv(SSM_K, ei, lambda k, jj=jj: pcol(l, "scw", jj * 4 + k, jj * 4 + k + 1), pcol(l, "scb", jj, jj + 1), ci)
                xbc_postB(22)
                xbc_postA(23, 23 % 3)
                xbc_postB(23)
                if kind == "S":
                    P.dma("sp", ("dma_start", dict(out=ossm_s[l], in_=ossm_t)), "o_ossm", r=[t_ossm])
                for t in range(NTT):
                    q = t % 4
                    for kt in range(KC):
                        P.op("pe", ("matmul", dict(out=banks[7][:, q * 128:q * 128 + 32], lhsT=hT[:, kt, t * 128:(t + 1) * 128], rhs=wdt_t[:, kt, :],
                                                   start=(kt == 0), stop=(kt == KC - 1))),
                             r=[t_hT.parts[kt], t_wdt], w=[t_bank[7].parts[q]])
                    P.op("dve", ("tensor_tensor", dict(out=dtraw[t][:], in0=banks[7][:, q * 128:q * 128 + 32], in1=pcol(l, "dtb"), op=ALU.add)),
                         r=[t_bank[7].parts[q], t_par], w=[t_dtraw[t]])
                P.barrier()

                if kind == "P":
                    if bi == 0:
                        P.op("dve", ("memset", dict(ap=hstate[:], constant=0.0)), w=[t_hstate])
                    else:
                        P.dma("sp", ("dma_start", dict(out=hstate[:], in_=ostate_p[l])), "hst_in", r=[t_ost[l]], w=[t_hstate])
                    for g in range(GROUPS):
                        P.op("act", ("copy", dict(out=hprev_b[:, g * 512:(g + 1) * 512], in_=hstate[:, g * 512:(g + 1) * 512])),
                             r=[t_hstate.parts[g]], w=[t_hprev.parts[g]])
                Tm = tri if kind == "P" else btri
                Sm = ones_f[:] if kind == "P" else same
                for t in range(NTT):
                    ssd_tile(l, t, Tm, Sm)
                if kind == "P":
                    P.dma("sp", ("dma_start", dict(out=ostate_p[l], in_=hstate[:])), "hst_out", r=[t_hstate], w=[t_ost[l]])
                P.barrier()

                for j in range(KC):
                    u = wload(w_in[l, 32 + j], cache=c_w_in[l, 32 + j])
                    b = j % 6
                    proj(b, [(u, 16)], hT_chunks(), n, rd_h)
                    ta = j % 3
                    tb = j % 2
                    P.op("act", ("activation", dict(out=tmpA[ta][:, 0:n], in_=banks[b][:, 0:n], func=AF.Silu)), r=[t_bank[b]], w=[t_tmpA[ta]])
                    P.op("dve", ("tensor_tensor", dict(out=BIG[:, 16 + j, :], in0=BIG[:, 16 + j, :], in1=tmpA[ta][:, 0:n], op=ALU.mult)),
                         r=[t_tmpA[ta], t_BIG.parts[16 + j]], w=[t_BIG.parts[16 + j]])
                    P.op("act", ("activation", dict(out=tmpB[tb][:, 0:n], in_=BIG[:, 16 + j, :], func=AF.Square)), r=[t_BIG.parts[16 + j]], w=[t_tmpB[tb]])
                    P.op("pe", ("matmul", dict(out=banks[6][:, 0:n], lhsT=ones_b[:], rhs=tmpB[tb][:, 0:n], start=(j == 0), stop=(j == KC - 1))),
                         r=[t_tmpB[tb], t_ones], w=[t_bank[6]])
                P.op("act", ("activation", dict(out=rstd[:, 0:n], in_=banks[6][:, 0:n], func=AF.Sqrt, scale=1.0 / D, bias=eps_t[:, 0:1])), r=[t_bank[6], t_ones], w=[t_rstd])
                P.op("dve", ("reciprocal", dict(out=rstd[:, 0:n], in_=rstd[:, 0:n])), r=[t_rstd], w=[t_rstd])
                for j in range(KC):
                    P.op("dve", ("scalar_tensor_tensor", dict(out=BIG[:, 16 + j, :], in0=BIG[:, 16 + j, :], scalar=pcol(l, "snw", j, j + 1), in1=rstd[:, 0:n],
                                                              op0=ALU.mult, op1=ALU.mult)),
                         r=[t_BIG.parts[16 + j], t_par, t_rstd], w=[t_BIG.parts[16 + j]])
                for m in range(KC):
                    u0 = wload(w_out[l, m, :, 0:16, :], cache=c_w_out[l, m, :, 0:16, :])
                    u1 = wload(w_out[l, m, :, 16:32, :], cache=c_w_out[l, m, :, 16:32, :])
                    b = m % 6
                    proj(b, [(u0, 16), (u1, 16)], [BIG[:, kt, :] for kt in range(32)], n, [t_BIG.parts[kt] for kt in range(32)])
                    resid_update(l, 32, m, b)

            def ssd_tile(l, t, Tm, Sm):
                tcs = slice(t * 128, (t + 1) * 128)
                xt_parts = xtok_parts(t)
                P.op("act", ("activation", dict(out=dta, in_=dtraw[t][:], func=AF.Abs)), r=[t_dtraw[t]], w=[t_dtv])
                P.op("act", ("activation", dict(out=dte, in_=dta, func=AF.Exp, scale=-1.0)), r=[t_dtv], w=[t_dtv])
                P.op("act", ("activation", dict(out=dte, in_=dte, func=AF.Ln, bias=1.0, scale=1.0)), r=[t_dtv], w=[t_dtv])
                P.op("dve", ("scalar_tensor_tensor", dict(out=dtv, in0=dtraw[t][:], scalar=0.0, in1=dte, op0=ALU.max, op1=ALU.add)),
                     r=[t_dtraw[t], t_dtv], w=[t_dtv])
                P.op("dve", ("tensor_tensor", dict(out=dtA, in0=dtv, in1=a_bc[l][:], op=ALU.mult)), r=[t_dtv, t_par], w=[t_dtA])
                P.op("pe", ("matmul", dict(out=banks[7][:, 0:32], lhsT=Tm, rhs=dtA, start=True, stop=True)), r=[t_cst, t_dtA], w=[t_bank[7].parts[0]])
                P.op("pe", ("matmul", dict(out=banks[7][:, 128:160], lhsT=Sm, rhs=dtA, start=True, stop=True)), r=[t_cst, t_ones, t_dtA], w=[t_bank[7].parts[1]])
                P.op("act", ("copy", dict(out=acs, in_=banks[7][:, 0:32])), r=[t_bank[7].parts[0]], w=[t_acs])
                P.op("dve", ("tensor_tensor", dict(out=wst, in0=banks[7][:, 128:160], in1=acs, op=ALU.subtract)), r=[t_bank[7].parts[1], t_acs], w=[t_wst])
                P.op("act", ("activation", dict(out=wst, in_=wst, func=AF.Exp)), r=[t_wst], w=[t_wst])
                P.op("dve", ("tensor_tensor", dict(out=wst, in0=wst, in1=dtv, op=ALU.mult)), r=[t_wst, t_dtv], w=[t_wst])
                P.op("dve", ("tensor_tensor", dict(out=xw.rearrange("p (h d) -> p h d", d=HP), in0=xtokv(t).rearrange("p (h d) -> p h d", d=HP),
                                                   in1=wst.unsqueeze(2).to_broadcast([128, HEADS, HP]), op=ALU.mult)),
                     r=xt_parts + [t_wst], w=[t_xw])
                if kind == "P":
                    P.op("act", ("activation", dict(out=dec_p, in_=banks[7][:, 128:160], func=AF.Exp)), r=[t_bank[7].parts[1]], w=[t_dec])
                else:
                    P.op("dve", ("tensor_tensor", dict(out=dtAz, in0=dtA.unsqueeze(1).to_broadcast([128, NS, 32]),
                                                       in1=seqm.unsqueeze(2).to_broadcast([128, NS, 32]), op=ALU.mult)),
                         r=[t_dtA, t_cst], w=[t_dtAz])
                    P.op("pe", ("matmul", dict(out=banks[6][:, 0:512], lhsT=ones_f[:], rhs=dtAz.rearrange("p b h -> p (b h)"), start=True, stop=True)),
                         r=[t_ones, t_dtAz], w=[t_bank[6]])
                    P.op("act", ("activation", dict(out=dec_s.rearrange("p b h -> p (b h)"), in_=banks[6][:, 0:512], func=AF.Exp)), r=[t_bank[6]], w=[t_decs])
                for g in range(GROUPS):
                    P.op("pe", ("matmul", dict(out=banks[6][:, g * 128:(g + 1) * 128], lhsT=BT[:, g, tcs], rhs=CT[:, g, tcs], start=True, stop=True)),
                         r=[t_BT.parts[g], t_CT.parts[g]], w=[t_bank[6].parts[g]])
                    P.op("dve", ("tensor_tensor", dict(out=cbm[:, g, :], in0=banks[6][:, g * 128:(g + 1) * 128], in1=Tm, op=ALU.mult)),
                         r=[t_bank[6].parts[g], t_cst], w=[t_cbm.parts[g]])
                if kind == "P":
                    for q in range(16):
                        g = q // 4
                        P.op("pe", ("matmul", dict(out=banks[q // 4][:, (q % 4) * 128:(q % 4 + 1) * 128], lhsT=hprev_b[:, q * 128:(q + 1) * 128],
                                                   rhs=CT[:, g, tcs], start=True, stop=True)),
                             r=[t_hprev.parts[g], t_CT.parts[g]], w=[t_bank[q // 4].parts[q % 4]])
                    for g in range(GROUPS):
                        P.op("pe", ("matmul", dict(out=banks[6][:, 0:512], lhsT=Btok[:, t, g * 128:(g + 1) * 128], rhs=xw[:, g * 512:(g + 1) * 512], start=True, stop=True)),
                             r=[t_Btok.parts[t], t_xw], w=[t_bank[6]])
                        hv = hstate[:, g * 512:(g + 1) * 512]
                        hv3 = hv.rearrange("p (h d) -> p h d", d=HP)
                        P.op("dve", ("tensor_tensor", dict(out=hv3, in0=hv3,
                                                           in1=dec_p[:, g * 8:(g + 1) * 8].unsqueeze(2).to_broadcast([128, 8, HP]), op=ALU.mult)),
                             r=[t_hstate.parts[g], t_dec], w=[t_hstate.parts[g]])
                        P.op("dve", ("tensor_tensor", dict(out=hv, in0=hv, in1=banks[6][:, 0:512], op=ALU.add)),
                             r=[t_hstate.parts[g], t_bank[6]], w=[t_hstate.parts[g]])
                        P.op("act", ("copy", dict(out=hprev_b[:, g * 512:(g + 1) * 512], in_=hv)),
                             r=[t_hstate.parts[g]], w=[t_hprev.parts[g]])
                else:
                    for b in range(NS):
                        hi = b % 2
                        P.dma("sp", ("dma_start", dict(out=h0f[hi], in_=h0T[l, b])), "h0f%d" % hi, w=[t_h0f[hi]])
                        P.op("act", ("copy", dict(out=h0b[hi], in_=h0f[hi])), r=[t_h0f[hi]], w=[t_h0b[hi]])
                        P.op("dve", ("tensor_scalar", dict(out=Bz[hi], in0=Btok[:, 0, :], scalar1=seqm[:, b:b + 1], scalar2=None, op0=ALU.mult)),
                             r=[t_Btok.parts[0], t_cst], w=[t_Bz[hi]])
                        for q in range(16):
                            g = q // 4
                            c0 = (q % 4) * 128 + b * TS
                            P.op("pe", ("matmul", dict(out=banks[q // 4][:, c0:c0 + TS],
                                                       lhsT=h0b[hi][:, q * 128:(q + 1) * 128], rhs=CT[:, g, b * TS:(b + 1) * TS], start=True, stop=True)),
                                 r=[t_h0b[hi], t_CT.parts[g]], w=[t_bank[q // 4].parts[q % 4]])
                        for g in range(GROUPS):
                            P.op("pe", ("matmul", dict(out=banks[6][:, 0:512], lhsT=Bz[hi][:, g * 128:(g + 1) * 128], rhs=xw[:, g * 512:(g + 1) * 512], start=True, stop=True)),
                                 r=[t_Bz[hi], t_xw], w=[t_bank[6]])
                            hv = h0f[hi][:, g * 512:(g + 1) * 512]
                            hv3 = hv.rearrange("p (h d) -> p h d", d=HP)
                            P.op("dve", ("tensor_tensor", dict(out=hv3, in0=hv3,
                                                               in1=dec_s[:, b, g * 8:(g + 1) * 8].unsqueeze(2).to_broadcast([128, 8, HP]), op=ALU.mult)),
                                 r=[t_h0f[hi].parts[g], t_decs, t_h0b[hi]], w=[t_h0f[hi].parts[g]])
                            P.op("dve", ("tensor_tensor", dict(out=hv, in0=hv, in1=banks[6][:, 0:512], op=ALU.add)),
                                 r=[t_h0f[hi].parts[g], t_bank[6]], w=[t_h0f[hi].parts[g]])
                        P.dma("sp", ("dma_start", dict(out=ostate_s[l, b], in_=h0f[hi])), "h0o%d" % hi, r=[t_h0f[hi]])
                P.op("dve", ("tensor_tensor", dict(out=xw.rearrange("p (h d) -> p h d", d=HP), in0=xtokv(t).rearrange("p (h d) -> p h d", d=HP),
                                                   in1=pcol(l, "dsk").unsqueeze(2).to_broadcast([128, HEADS, HP]), op=ALU.mult)),
                     r=xt_parts + [t_par], w=[t_xw])
                def st1(q):
                    g = q // 4
                    pi = q % 2
                    for hl in range(2):
                        h = 2 * q + hl
                        si = h % 4
                        aq = q % 4
                        ab = 4 + hl
                        aqs = slice(aq * 128, (aq + 1) * 128)
                        P.op("pe", ("matmul", dict(out=banks[ab][:, aqs], lhsT=dtA[:, h:h + 1].to_broadcast([128, 128]), rhs=Tm, start=True, stop=True)),
                             r=[t_dtA, t_cst], w=[t_bank[ab].parts[aq]])
                    for hl in range(2):
                        h = 2 * q + hl
                        si = h % 4
                        aq = q % 4
                        ab = 4 + hl
                        aqs = slice(aq * 128, (aq + 1) * 128)
                        if hl == 0:
                            P.op("dve", ("tensor_scalar", dict(out=seg[si], in0=banks[ab][:, aqs], scalar1=acs[:, h:h + 1], scalar2=0.0,
                                                               op0=ALU.subtract, op1=ALU.min)),
                                 r=[t_bank[ab].parts[aq], t_acs], w=[t_seg[si]])
                            P.op("act", ("activation", dict(out=Ep[pi][hl * 64:(hl + 1) * 64, :], in_=banks[ab][hl * 64:(hl + 1) * 64, aqs], func=AF.Exp)),
                                 r=[t_bank[ab].parts[aq]], w=[t_Ep[pi].parts[hl]])
                            P.op("act", ("activation", dict(out=seg[si], in_=seg[si], func=AF.Exp)), r=[t_seg[si]], w=[t_seg[si]])
                        else:
                            P.op("act", ("activation", dict(out=seg[si], in_=banks[ab][:, aqs], func=AF.Relu, scale=-1.0, bias=acs[:, h:h + 1])),
                                 r=[t_bank[ab].parts[aq], t_acs], w=[t_seg[si]])
                            P.op("act", ("activation", dict(out=Ep[pi][hl * 64:(hl + 1) * 64, :], in_=banks[ab][hl * 64:(hl + 1) * 64, aqs], func=AF.Exp)),
                                 r=[t_bank[ab].parts[aq]], w=[t_Ep[pi].parts[hl]])
                            P.op("act", ("activation", dict(out=seg[si], in_=seg[si], func=AF.Exp, scale=-1.0)), r=[t_seg[si]], w=[t_seg[si]])

                def st2(q):
                    g = q // 4
                    pi = q % 2
                    yb = 6 + q % 2
                    yq = ((q // 2) % 2) * 2
                    ysl = slice(yq * 128, yq * 128 + 256)
                    for hl in range(2):
                        h = 2 * q + hl
                        si = h % 4
                        P.op("dve", ("scalar_tensor_tensor", dict(out=Mp[pi][:, hl, :], in0=seg[si], scalar=dtv[:, h:h + 1], in1=cbm[:, g, :], op0=ALU.mult, op1=ALU.mult)),
                             r=[t_seg[si], t_dtv, t_cbm.parts[g]], w=[t_Mp[pi].parts[hl]])
                    P.op("pe", ("matmul", dict(out=banks[yb][:, ysl], lhsT=xtokv(t)[:, q * 128:(q + 1) * 128],
                                               rhs=Mp[pi].rearrange("p a b -> p (a b)"), start=True, stop=False)),
                         r=xtok_parts(t, q * 128, (q + 1) * 128) + [t_Mp[pi]], w=[t_bank[yb].parts[yq], t_bank[yb].parts[yq + 1]])
                    P.op("pe", ("matmul", dict(out=banks[yb][:, ysl], lhsT=xw[:, q * 128:(q + 1) * 128],
                                               rhs=ident_b[:].rearrange("p a b -> p (a b)"), start=False, stop=True)),
                         r=[t_xw, t_ones], w=[t_bank[yb].parts[yq], t_bank[yb].parts[yq + 1]])
                    P.op("dve", ("tensor_tensor", dict(out=toff[pi], in0=banks[q // 4][:, (q % 4) * 128:(q % 4 + 1) * 128], in1=Ep[pi], op=ALU.mult)),
                         r=[t_bank[q // 4].parts[q % 4], t_Ep[pi]], w=[t_toff[pi]])

                def st3(q):
                    pi = q % 2
                    yb = 6 + q % 2
                    yq = ((q // 2) % 2) * 2
                    for hl in range(2):
                        ps = slice(hl * 64, (hl + 1) * 64)
                        P.op("dve", ("tensor_tensor", dict(out=BIG[ps, 16 + q, tcs], in0=banks[yb][ps, (yq + hl) * 128:(yq + hl + 1) * 128],
                                                           in1=toff[pi][ps, :], op=ALU.add)),
                             r=[t_bank[yb].parts[yq + hl], t_toff[pi]], w=[t_BIG.parts[16 + q]])

                for it in range(16 + 2):
                    if it < 16:
                        st1(it)
                    if 0 <= it - 1 < 16:
                        st2(it - 1)
                    if 0 <= it - 2 < 16:
                        st3(it - 2)

            def ffn(l):
                rms_stats([xT[:, j, :] for j in range(KC)], t_xT)
                for j in range(KC):
                    modulate(l, Wf[l], 48, j, hT[:, j, :], t_hT.parts[j])
                rd_h = [t_hT.parts[kt] for kt in range(KC)]
                if kind == "S":
                    P.dma("sp", ("dma_start", dict(out=hf_t, in_=hffn[l])), "hf", w=[t_hf])
                pend = [None]

                def post(j, c1_, c2_):
                    ta = j % 3
                    P.op("act", ("activation", dict(out=tmpA[ta][:, 0:n], in_=cacc[c2_][:, 0:n], func=AF.Silu)), r=[t_cacc[c2_]], w=[t_tmpA[ta]])
                    P.op("dve", ("tensor_tensor", dict(out=BIG[:, j, :], in0=cacc[c1_][:, 0:n], in1=tmpA[ta][:, 0:n], op=ALU.mult)),
                         r=[t_cacc[c1_], t_tmpA[ta]], w=[t_BIG.parts[j]])

                cnt = 0
                for j in range(FKC):
                    u1 = wload(w_up[l, j], cache=c_w_up[l, j])
                    u2 = wload(w_up[l, FKC + j], cache=c_w_up[l, FKC + j])
                    b1, b2 = (0, 1) if j % 3 == 0 else ((2, 3) if j % 3 == 1 else (4, 5))
                    proj(b1, [(u1, 16)], hT_chunks(), n, rd_h)
                    proj(b2, [(u2, 16)], hT_chunks(), n, rd_h)
                    cs = []
                    for (jj, b) in ((j, b1), (FKC + j, b2)):
                        ei = cnt % 3
                        cnt += 1
                        dw_load(FFN_K, ei, b, halo_f[l][:, jj, :], t_halo_f[l].parts[jj], hf_t[:, jj] if kind == "S" else None, t_hf if kind == "S" else None)
                        if kind == "P":
                            P.op("act", ("copy", dict(out=halo_f[l][:, jj, :], in_=ext[ei][:, n:n + 2])), r=[t_ext[ei]], w=[t_halo_f[l].parts[jj]])
                        else:
                            ev = ext[ei][:, 0:NS * 10].rearrange("p (b w) -> p b w", w=10)
                            P.op("act", ("copy", dict(out=offn_t[:, jj], in_=ev[:, :, 8:10])), r=[t_ext[ei]], w=[t_offn.parts[jj]])
                        cs.append((ei, jj))
                    if pend[0] is not None:
                        pend[0]()
                    for (ei, jj) in cs:
                        dw_conv(FFN_K, ei, lambda k, jj=jj: pcol(l, "fcw", jj * 3 + k, jj * 3 + k + 1), pcol(l, "fcb", jj, jj + 1), ei)
                    pend[0] = (lambda j=j, c1_=cs[0][0], c2_=cs[1][0]: post(j, c1_, c2_))
                pend[0]()
                if kind == "S":
                    P.dma("sp", ("dma_start", dict(out=offn_s[l], in_=offn_t)), "o_offn", r=[t_offn])
                for m in range(KC):
                    u0 = wload(w_dn[l, m, :, 0:16, :], cache=c_w_dn[l, m, :, 0:16, :])
                    u1 = wload(w_dn[l, m, :, 16:32, :], cache=c_w_dn[l, m, :, 16:32, :])
                    u2 = wload(w_dn[l, m, :, 32:44, :], kt=12, cache=c_w_dn[l, m, :, 32:44, :])
                    b = m % 6
                    proj(b, [(u0, 16), (u1, 16), (u2, 12)], [BIG[:, kt, :] for kt in range(FKC)], n, [t_BIG.parts[kt] for kt in range(FKC)])
                    resid_update(l, 80, m, b)

            def run_block(bi):
                if kind == "P":
                    P.dma("sp", ("dma_start", dict(out=xT, in_=xTp[:, :, bi * NBP:(bi + 1) * NBP])), "xin", w=[t_xT])
                else:
                    P.dma("sp", ("dma_start", dict(out=xT, in_=xTs)), "xin", w=[t_xT])
                for l in range(DEPTH):
                    if kind == "P" and bi == 0 and l == 1:
                        compute_mod(1)
                    mixer(l, bi)
                    ffn(l)
                rms_stats([xT[:, j, :] for j in range(KC)], t_xT)
                for j in range(KC):
                    P.op("dve", ("scalar_tensor_tensor", dict(out=xT[:, j, :], in0=xT[:, j, :], scalar=fnw_t[:, j:j + 1], in1=rstd[:, 0:n], op0=ALU.mult, op1=ALU.mult)),
                         r=[t_xT.parts[j], t_par, t_rstd], w=[t_xT.parts[j]])
                if kind == "P":
                    P.dma("sp", ("dma_start", dict(out=yTp[:, :, bi * NBP:(bi + 1) * NBP], in_=xT)), "xout", r=[t_xT])
                else:
                    P.dma("sp", ("dma_start", dict(out=yTs, in_=xT)), "xout", r=[t_xT])

            return run_block

        run_p = make_block_fns("P")
        for bi in range(NPB):
            run_p(bi)
            if bi == 0:
                P.barrier()
                wmode["first"] = False
        for l in range(DEPTH):
            P.dma("sp", ("dma_start", dict(out=oconf_p[l], in_=halo_c32[l][:])), "o_fin", r=[t_halo_c32[l]])
            P.dma("sp", ("dma_start", dict(out=ossm_p[l], in_=halo_s[l][:])), "o_fin", r=[t_halo_s[l]])
            P.dma("sp", ("dma_start", dict(out=offn_p[l], in_=halo_f[l][:])), "o_fin", r=[t_halo_f[l]])
        P.barrier()
        run_s = make_block_fns("S")
        run_s(0)
        fin_deps = [("d", k, v) for k, v in P.dmacnt.items() if k.startswith("o_") or k.startswith("xout") or k.startswith("h0o") or k == "hst_out"]
        P.streams["sp"].append((None, fin_deps, None))

        P.emit(es)
    return nc


def _fm(a):
    C = a.shape[-1]
    kc = C // 128
    lead = a.shape[:-1]
    b = a.reshape(lead + (kc, 128))
    nd = b.ndim
    perm = (nd - 1, nd - 2) + tuple(range(nd - 2))
    return np.ascontiguousarray(b.transpose(perm))


def _block_w(w, mt=None):
    K, N = w.shape
    return np.ascontiguousarray(w.reshape(K // 128, 128, N // 128, 128).transpose(2, 1, 0, 3))


def _consts():
    c = np.zeros((128, NCONST), np.float32)
    i = np.arange(128)
    c[:, C_ID:C_ID + 128] = np.eye(128, dtype=np.float32)
    c[:, C_TRI:C_TRI + 128] = (i[:, None] <= i[None, :]).astype(np.float32)
    sameseq = (i[:, None] // TS == i[None, :] // TS)
    c[:, C_BTRI:C_BTRI + 128] = (sameseq & (i[:, None] <= i[None, :])).astype(np.float32)
    c[:, C_SAME:C_SAME + 128] = sameseq.astype(np.float32)
    c[:, C_SEQM:C_SEQM + 16] = (i[:, None] // TS == np.arange(16)[None, :]).astype(np.float32)
    return c


_NC_CACHE = {}


def kernel(x_prompt, x_sample, c_prompt, c_sample, state_conf_conv, state_ssm_conv, state_ssm,
           state_ffn_conv, ada_w, ada_b, norm_mix_w, norm_ffn_w, w_in, conf_dw_w, conf_dw_b,
           conf_ln_w, conf_ln_b, ssm_conv_w, ssm_conv_b, dt_bias, a_log, d_skip, ssm_norm_w,
           w_out, ffn_w_up, ffn_dw_w, ffn_dw_b, ffn_w_down, final_norm_w):
    f = lambda a: np.asarray(a, dtype=np.float32)
    x_prompt, x_sample, c_prompt, c_sample = f(x_prompt), f(x_sample), f(c_prompt), f(c_sample)
    state_conf_conv, state_ssm_conv, state_ssm, state_ffn_conv = f(state_conf_conv), f(state_ssm_conv), f(state_ssm), f(state_ffn_conv)
    ada_w, ada_b, w_in, w_out, ffn_w_up, ffn_w_down = f(ada_w), f(ada_b), f(w_in), f(w_out), f(ffn_w_up), f(ffn_w_down)
    L = DEPTH

    def cols(v):
        return np.ascontiguousarray(f(v).reshape(-1, 128).T)

    par = np.zeros((L, 128, NPAR), np.float32)

    def put(l, name, arr):
        o, w = _off[name]
        par[l, :, o:o + w] = arr.reshape(128, w)

    for l in range(L):
        put(l, "nwm", cols(norm_mix_w[l]))
        put(l, "nwf", cols(norm_ffn_w[l]))
        put(l, "cw", np.ascontiguousarray(f(conf_dw_w[l]).reshape(31, 16, 128).transpose(2, 1, 0)))
        put(l, "cb", cols(conf_dw_b[l]))
        put(l, "lnw", cols(conf_ln_w[l]))
        put(l, "lnb", cols(conf_ln_b[l]))
        put(l, "scw", np.ascontiguousarray(f(ssm_conv_w[l]).reshape(4, 24, 128).transpose(2, 1, 0)))
        put(l, "scb", cols(ssm_conv_b[l]))
        put(l, "dtb", np.broadcast_to(f(dt_bias[l])[None, :], (128, 32)))
        put(l, "alog", np.broadcast_to(f(a_log[l])[None, :], (128, 32)))
        put(l, "dsk", np.broadcast_to(f(d_skip[l])[None, :], (128, 32)))
        put(l, "snw", cols(ssm_norm_w[l]))
        put(l, "fcw", np.ascontiguousarray(f(ffn_dw_w[l]).reshape(3, 88, 128).transpose(2, 1, 0)))
        put(l, "fcb", cols(ffn_dw_b[l]))
        put(l, "adab", cols(ada_b[l]))
    fnw = cols(final_norm_w)
    w_in_b = np.stack([_block_w(w_in[l][:, :9216]) for l in range(L)])
    w_dt = np.stack([np.ascontiguousarray(w_in[l][:, 9216:].reshape(16, 128, 32).transpose(1, 0, 2)) for l in range(L)])
    w_out_b = np.stack([_block_w(w_out[l]) for l in range(L)])
    w_up_b = np.stack([_block_w(ffn_w_up[l]) for l in range(L)])
    w_dn_b = np.stack([_block_w(ffn_w_down[l]) for l in range(L)])
    ada_blk = np.stack([_block_w(ada_w[l]) for l in range(L)])
    consts = _consts()

    in_maps = []
    for c in range(8):
        s = c % 4
        sl = slice(NS * c, NS * (c + 1))
        cc = np.concatenate([c_prompt[s][None], c_sample[sl]], axis=0)
        m = {
            "xTp": _fm(x_prompt[s]),
            "xTs": _fm(x_sample[sl].reshape(NS * TS, D)),
            "cT": _fm(cc),
            "consts": consts, "par": par, "fnw": fnw,
            "w_in": w_in_b, "w_dt": w_dt, "w_out": w_out_b, "w_up": w_up_b, "w_dn": w_dn_b, "ada": ada_blk,
            "hconf": np.stack([_fm(state_conf_conv[l, sl]) for l in range(L)]),
            "hssm": np.stack([_fm(state_ssm_conv[l, sl]) for l in range(L)]),
            "hffn": np.stack([_fm(state_ffn_conv[l, sl]) for l in range(L)]),
            "h0T": np.ascontiguousarray(state_ssm[:, sl].reshape(L, NS, HEADS * HP, NST).transpose(0, 1, 3, 2)),
        }
        in_maps.append(m)

    if "nc" not in _NC_CACHE:
        _NC_CACHE["nc"] = build_program()
    nc = _NC_CACHE["nc"]
    res = run_bass_kernel_spmd(nc, in_maps, core_ids=list(range(8)))
    R = res.results

    def unfm(a):
        nd = a.ndim
        perm = tuple(range(2, nd)) + (1, 0)
        b = a.transpose(perm)
        return np.ascontiguousarray(b).reshape(b.shape[:-2] + (b.shape[-2] * 128,))

    y_prompt = np.stack([unfm(R[s]["yTp"]) for s in range(4)])
    y_sample = np.concatenate([unfm(R[c]["yTs"]).reshape(NS, TS, D) for c in range(8)], axis=0)
    p_conf = np.stack([np.stack([unfm(R[s]["oconf_p"][l]) for s in range(4)]) for l in range(L)])
    p_sconv = np.stack([np.stack([unfm(R[s]["ossm_p"][l]) for s in range(4)]) for l in range(L)])
    p_ssm = np.stack([np.stack([R[s]["ostate_p"][l].T.reshape(HEADS, HP, NST) for s in range(4)]) for l in range(L)])
    p_ffn = np.stack([np.stack([unfm(R[s]["offn_p"][l]) for s in range(4)]) for l in range(L)])
    s_conf_l, s_sconv_l, s_ssm_l, s_ffn_l = [], [], [], []
    for l in range(L):
        cf, sc, ss, ff = [], [], [], []
        for c in range(8):
            hist = unfm(R[c]["ohist_s"][l])
            new = unfm(R[c]["onew_s"][l]).reshape(NS, TS, D_CONV)
            cf.append(np.concatenate([hist[:, TS:], new], axis=1))
            sc.append(unfm(R[c]["ossm_s"][l]))
            ss.append(R[c]["ostate_s"][l].transpose(0, 2, 1).reshape(NS, HEADS, HP, NST))
            ff.append(unfm(R[c]["offn_s"][l]))
        s_conf_l.append(np.concatenate(cf, axis=0))
        s_sconv_l.append(np.concatenate(sc, axis=0))
        s_ssm_l.append(np.concatenate(ss, axis=0))
        s_ffn_l.append(np.concatenate(ff, axis=0))
    outs = (y_prompt, y_sample, p_conf, p_sconv, p_ssm, p_ffn,
            np.stack(s_conf_l), np.stack(s_sconv_l), np.stack(s_ssm_l), np.stack(s_ffn_l))
    return tuple(np.ascontiguousarray(o, dtype=np.float32) for o in outs)
```
